# Optimizing a Trainium2 kernel written in Bass

```python
import jax, jax.numpy as jnp
from jax import lax
import numpy as np

D_MODEL = 1024
BATCH = 2
SEQ = 8192
DEPTH = 4
DEC_BATCH = 32
DEC_SEQ = 1
PAST_LEN = 8192
PAGE_SIZE = 128

POOL_WINDOWS = (2, 4, 8, 16)
N_POOL_GROUPS = len(POOL_WINDOWS)
POOL_GC = D_MODEL // N_POOL_GROUPS
POOL_STATE = max(POOL_WINDOWS) - 1
ATTN_PATTERNS = ((128, 1), (512, 4), (2048, 16))
N_ATTN_GROUPS = len(ATTN_PATTERNS)
HEAD_DIM = 64
HEADS_PER_GROUP = 8
ATTN_OUT = HEADS_PER_GROUP * HEAD_DIM
QKV_WIDTH = 3 * N_ATTN_GROUPS * ATTN_OUT
Q_BLOCK = 128
ROPE_THETA = 10000.0
D_FF = 4 * D_MODEL
EPS = 1e-6
NEG = -1e30
N_POOL_LAYERS = (DEPTH + 1) // 2
N_ATTN_LAYERS = DEPTH // 2

kernel_name = "hybrid_pool_dilated_attn_decoder_step"


def rmsnorm(x, g):
    xf = x.astype(jnp.float32)
    y = xf * lax.rsqrt(jnp.mean(xf * xf, axis=-1, keepdims=True) + EPS)
    return (y * g.astype(jnp.float32)).astype(x.dtype)


def rope(x, pos):
    half = HEAD_DIM // 2
    inv = ROPE_THETA ** (-jnp.arange(half, dtype=jnp.float32) * 2.0 / HEAD_DIM)
    ang = pos.astype(jnp.float32)[:, None] * inv[None, :]
    cos = jnp.cos(ang)[None, :, None, None, :]
    sin = jnp.sin(ang)[None, :, None, None, :]
    xf = x.astype(jnp.float32)
    x1, x2 = xf[..., :half], xf[..., half:]
    return jnp.concatenate([x1 * cos - x2 * sin, x2 * cos + x1 * sin], axis=-1).astype(x.dtype)


def pool_mix(xn, prev, p0, w_pool, scale):
    B, S, _ = xn.shape
    ext = jnp.concatenate([prev.astype(xn.dtype), xn], axis=1)
    cs = jnp.cumsum(ext.astype(jnp.float32), axis=1)
    cs = jnp.concatenate([jnp.zeros((B, 1, D_MODEL), jnp.float32), cs], axis=1)
    pos = p0 + jnp.arange(S)
    P = POOL_STATE
    xf = xn.astype(jnp.float32)
    diffs = []
    for g, w in enumerate(POOL_WINDOWS):
        sl = slice(g * POOL_GC, (g + 1) * POOL_GC)
        hi = cs[:, P + 1:P + 1 + S, sl]
        lo = cs[:, P + 1 - w:P + 1 - w + S, sl]
        cnt = jnp.minimum(pos + 1, w).astype(jnp.float32)[None, :, None]
        diffs.append((hi - lo) / cnt - xf[..., sl])
    d = jnp.stack(diffs, axis=2)
    y = jnp.einsum('bsgc,gce->bsge', d, w_pool.astype(jnp.float32)).reshape(B, S, D_MODEL)
    y = (y * scale.astype(jnp.float32)).astype(xn.dtype)
    return y, ext[:, ext.shape[1] - P:]


def dilated_group(q, ext, p0, n_buf, window, dilation):
    B, S, H, Dh = q.shape
    n_keys = window // dilation + 1
    Q = Q_BLOCK if S % Q_BLOCK == 0 else S
    nb = S // Q
    dist = jnp.arange(n_keys) * dilation
    qi = jnp.arange(Q)
    scale = HEAD_DIM ** -0.5
    band_len = n_buf + Q

    def block(t0):
        qb = lax.dynamic_slice_in_dim(q, t0, Q, axis=1).astype(jnp.float32)
        band = lax.dynamic_slice_in_dim(ext, t0, band_len, axis=1)
        idx = n_buf + qi[:, None] - dist[None, :]
        valid = ((p0 + t0 + qi[:, None] - dist[None, :]) >= 0) & (idx >= 0)
        kv = jnp.take(band, jnp.clip(idx, 0, band_len - 1).reshape(-1), axis=1)
        kv = kv.reshape(B, Q, n_keys, 2, H, Dh).astype(jnp.float32)
        s = jnp.einsum('bqhd,bqkhd->bqhk', qb, kv[:, :, :, 0]) * scale
        s = jnp.where(valid[None, :, None, :], s, NEG)
        m = jnp.max(s, axis=-1)
        p = jnp.exp(s - m[..., None])
        l = jnp.sum(p, axis=-1)
        o = jnp.einsum('bqhk,bqkhd->bqhd', p, kv[:, :, :, 1])
        return o, m, l

    o, m, l = lax.map(block, jnp.arange(nb) * Q)
    o = jnp.transpose(o, (1, 0, 2, 3, 4)).reshape(B, S, H, Dh)
    m = jnp.transpose(m, (1, 0, 2, 3)).reshape(B, S, H)
    l = jnp.transpose(l, (1, 0, 2, 3)).reshape(B, S, H)
    return o, m, l


def attn_mix(xn, bufs, out_lens, p0, w_qkv, q_norm, k_norm, w_o):
    B, S, _ = xn.shape
    pos = p0 + jnp.arange(S)
    qkv = jnp.einsum('bsd,de->bse', xn, w_qkv).reshape(B, S, 3, N_ATTN_GROUPS, HEADS_PER_GROUP, HEAD_DIM)
    q = rope(rmsnorm(qkv[:, :, 0], q_norm[:, None, :]), pos)
    k = rope(rmsnorm(qkv[:, :, 1], k_norm[:, None, :]), pos)
    v = qkv[:, :, 2]
    kv_new = jnp.stack([k, v], axis=2)
    outs, ms, ls, new_bufs = [], [], [], []
    for g, (window, dilation) in enumerate(ATTN_PATTERNS):
        buf = bufs[g]
        ext = jnp.concatenate([buf, kv_new[:, :, :, g].astype(buf.dtype)], axis=1)
        o, m, l = dilated_group(q[:, :, g], ext, p0, buf.shape[1], window, dilation)
        outs.append(o); ms.append(m); ls.append(l)
        new_bufs.append(ext[:, ext.shape[1] - out_lens[g]:])
    m_all = jnp.stack(ms)
    wgt = jnp.exp(m_all - jnp.max(m_all, axis=0, keepdims=True))
    den = jnp.sum(wgt * jnp.stack(ls), axis=0)
    num = jnp.sum(wgt[..., None] * jnp.stack(outs), axis=0)
    out = (num / den[..., None]).reshape(B, S, ATTN_OUT).astype(xn.dtype)
    return jnp.einsum('bse,ed->bsd', out, w_o), new_bufs


def trunk(x, pool_prev, kv_prev, out_lens, p0, norm_mix, norm_mlp, pool_w, pool_scale,
          attn_w_qkv, attn_q_norm, attn_k_norm, attn_w_o, mlp_w_up, mlp_w_down):
    pool_new, kv_new = [], []
    for layer in range(DEPTH):
        li = layer // 2
        h = rmsnorm(x, norm_mix[layer])
        if layer % 2 == 0:
            y, st = pool_mix(h, pool_prev[li], p0, pool_w[li], pool_scale[li])
            pool_new.append(st)
        else:
            y, bufs = attn_mix(h, [c[li] for c in kv_prev], out_lens, p0,
                               attn_w_qkv[li], attn_q_norm[li], attn_k_norm[li], attn_w_o[li])
            kv_new.append(bufs)
        x = x + y
        h = rmsnorm(x, norm_mlp[layer])
        x = x + jnp.einsum('bsf,fd->bsd', jnp.square(jax.nn.relu(jnp.einsum('bsd,df->bsf', h, mlp_w_up[layer]))),
                           mlp_w_down[layer])
    pool_st = jnp.stack(pool_new)
    kv_st = [jnp.stack([kv_new[l][g] for l in range(len(kv_new))]) for g in range(N_ATTN_GROUPS)]
    return x, pool_st, kv_st


def setup_inputs(seed: int = 0) -> dict:
    key = jax.random.key(seed)
    ks = jax.random.split(key, 16)
    f32 = jnp.float32
    def nrm(k, shape, s):
        return jax.random.normal(k, shape, f32) * s
    n_buf = [min(w, PAST_LEN) for w, _ in ATTN_PATTERNS]
    return {
        "x_prompt": nrm(ks[0], (BATCH, SEQ, D_MODEL), 1.0),
        "x_sample": nrm(ks[1], (DEC_BATCH, DEC_SEQ, D_MODEL), 1.0),
        "state_pool": nrm(ks[2], (N_POOL_LAYERS, DEC_BATCH, POOL_STATE, D_MODEL), 1.0),
        "cache_kv_w128": nrm(ks[3], (N_ATTN_LAYERS, DEC_BATCH, n_buf[0], 2, HEADS_PER_GROUP, HEAD_DIM), 1.0),
        "cache_kv_w512": nrm(ks[4], (N_ATTN_LAYERS, DEC_BATCH, n_buf[1], 2, HEADS_PER_GROUP, HEAD_DIM), 1.0),
        "cache_kv_w2048": nrm(ks[5], (N_ATTN_LAYERS, DEC_BATCH, n_buf[2], 2, HEADS_PER_GROUP, HEAD_DIM), 1.0),
        "norm_mix": 1.0 + nrm(ks[6], (DEPTH, D_MODEL), 0.05),
        "norm_mlp": 1.0 + nrm(ks[7], (DEPTH, D_MODEL), 0.05),
        "pool_w": nrm(ks[8], (N_POOL_LAYERS, N_POOL_GROUPS, POOL_GC, POOL_GC), POOL_GC ** -0.5),
        "pool_scale": 1.0 + nrm(ks[9], (N_POOL_LAYERS, D_MODEL), 0.1),
        "attn_w_qkv": nrm(ks[10], (N_ATTN_LAYERS, D_MODEL, QKV_WIDTH), D_MODEL ** -0.5),
        "attn_q_norm": 1.0 + nrm(ks[11], (N_ATTN_LAYERS, N_ATTN_GROUPS, HEAD_DIM), 0.05),
        "attn_k_norm": 1.0 + nrm(ks[12], (N_ATTN_LAYERS, N_ATTN_GROUPS, HEAD_DIM), 0.05),
        "attn_w_o": nrm(ks[13], (N_ATTN_LAYERS, ATTN_OUT, D_MODEL), ATTN_OUT ** -0.5),
        "mlp_w_up": nrm(ks[14], (DEPTH, D_MODEL, D_FF), D_MODEL ** -0.5),
        "mlp_w_down": nrm(ks[15], (DEPTH, D_FF, D_MODEL), 0.5 * D_FF ** -0.5),
    }


def reference(x_prompt, x_sample, state_pool, cache_kv_w128, cache_kv_w512, cache_kv_w2048,
              norm_mix, norm_mlp, pool_w, pool_scale, attn_w_qkv, attn_q_norm, attn_k_norm,
              attn_w_o, mlp_w_up, mlp_w_down):
    weights = (norm_mix, norm_mlp, pool_w, pool_scale, attn_w_qkv, attn_q_norm, attn_k_norm,
               attn_w_o, mlp_w_up, mlp_w_down)
    B, S, _ = x_prompt.shape
    pool_prev_p = jnp.zeros((N_POOL_LAYERS, B, POOL_STATE, D_MODEL), x_prompt.dtype)
    kv_prev_p = [jnp.zeros((N_ATTN_LAYERS, B, w, 2, HEADS_PER_GROUP, HEAD_DIM), x_prompt.dtype)
                 for w, _ in ATTN_PATTERNS]
    out_lens_p = tuple(min(w, S) for w, _ in ATTN_PATTERNS)
    y_prompt, pool_p, kv_p = trunk(x_prompt, pool_prev_p, kv_prev_p, out_lens_p, 0, *weights)
    kv_prev_s = [cache_kv_w128, cache_kv_w512, cache_kv_w2048]
    out_lens_s = tuple(c.shape[2] for c in kv_prev_s)
    y_sample, pool_s, kv_s = trunk(x_sample, state_pool, kv_prev_s, out_lens_s, PAST_LEN, *weights)
    return (y_prompt, y_sample, pool_p, pool_s, kv_p[0], kv_s[0], kv_p[1], kv_s[1], kv_p[2], kv_s[2])
```

```python
import numpy as np
import ml_dtypes
import concourse.bass as bass
import concourse.mybir as mybir
from concourse.bass_utils import run_bass_kernel_spmd

F32 = mybir.dt.float32
BF16 = mybir.dt.bfloat16
AF = mybir.ActivationFunctionType
ALU = mybir.AluOpType
AX = mybir.AxisListType

NCORES = 8
D = 1024
FF = 4096
SEQ = 8192
TC = 2048
NS = 4
EPS = 1e-6
DIL = [1, 4, 16]
NBUF = [128, 512, 2048]
WIN = [2, 4, 8, 16]
ENGS = ["pe", "act", "dve", "pool", "sp"]
import os
DBG_LAYERS = int(os.environ.get("KDBG_LAYERS", "4"))
DBG_SKIP_LAST_MLP = int(os.environ.get("KDBG_SKIPMLP", "0"))
DBG_CHUNKS = int(os.environ.get("KDBG_CHUNKS", "4"))
DBG_SAMPLE = int(os.environ.get("KDBG_SAMPLE", "1"))
DBG_DUMP = int(os.environ.get("KDBG_DUMP", "0"))
EPOCH = 8000


def ap_bcast_last(a, n):
    return bass.AP(a.tensor, a.offset, [list(x) for x in a.ap] + [[0, n]])


def ap_bcast_mid(a, n):
    l = [list(x) for x in a.ap]
    return bass.AP(a.tensor, a.offset, [l[0], [0, n]] + l[1:])


class Prog:
    def __init__(self, nc):
        self.nc = nc
        self.q = {e: [] for e in ENGS}
        self.cnt = {e: 0 for e in ENGS}
        self.waited = {e: {} for e in ENGS}
        self.lastw = {}
        self.readers = {}
        self.dmacnt = {}
        self.sems = {}
        self.pend_r = {e: set() for e in ENGS}
        self.pend_w = {e: set() for e in ENGS}
        self.last_tok = {e: None for e in ENGS}

    def sem(self, key):
        if key not in self.sems:
            self.sems[key] = self.nc.alloc_semaphore(name="s%d" % len(self.sems))
        return self.sems[key]

    def _need(self, eng, tok, is_dma):
        if tok is None:
            return None
        key, val, teng, tdma = tok
        if (not is_dma) and (not tdma) and teng == eng and eng == "pe":
            return None
        if self.waited[eng].get(key, 0) >= val:
            return None
        self.waited[eng][key] = val
        return (key, val)

    def op(self, eng, fn, reads=(), writes=(), signal=True, dma=False, nowaw=False):
        waits = []
        for b in reads:
            w = self._need(eng, self.lastw.get(b), dma)
            if w:
                waits.append(w)
        for b in writes:
            if nowaw:
                continue
            w = self._need(eng, self.lastw.get(b), dma)
            if w:
                waits.append(w)
            for t in self.readers.get(b, ()):
                w = self._need(eng, t, dma)
                if w:
                    waits.append(w)
        tok = None
        if dma:
            assert len(writes) == 1
            wb = writes[0]
            self.dmacnt[wb] = self.dmacnt.get(wb, 0) + 1
            tok = (("dma", wb), 16 * self.dmacnt[wb], eng, True)
        elif signal:
            self.cnt[eng] += 1
            ep = (self.cnt[eng] - 1) // EPOCH
            tok = (("eng", eng, ep), self.cnt[eng] - ep * EPOCH, eng, False)
            self.last_tok[eng] = tok
        wl = [(self.sem(k), v) for k, v in waits]
        ts = (self.sem(tok[0]), 16 if dma else 1) if tok else None

        def emit(e, wl=wl, fn=fn, ts=ts):
            for s, v in wl:
                e.wait_ge(s, v)
            ins = fn(e)
            if ts is not None:
                ins.then_inc(ts[0], ts[1])
        self.q[eng].append(emit)
        if tok is None:
            self.pend_r[eng].update(reads)
            self.pend_w[eng].update(writes)
        else:
            rs = set(reads)
            ws = set(writes)
            if not dma:
                rs |= self.pend_r[eng]
                ws |= self.pend_w[eng]
                self.pend_r[eng] = set()
                self.pend_w[eng] = set()
            for b in ws:
                self.lastw[b] = tok
                self.readers[b] = []
            for b in rs:
                if b not in ws:
                    self.readers.setdefault(b, []).append(tok)

    def barrier(self, keep=(), final=False):
        toks = {}
        for e in ENGS:
            t = self.last_tok[e]
            if t is not None:
                toks[t[0]] = max(toks.get(t[0], (0,))[0], t[1]), t
        for b, t in self.lastw.items():
            if t[3] and (final or b != "OUTC"):
                toks[t[0]] = max(toks.get(t[0], (0,))[0], t[1]), t
        for b, l in self.readers.items():
            for t in l:
                if t[3]:
                    toks[t[0]] = max(toks.get(t[0], (0,))[0], t[1]), t
        for e in ENGS:
            wl = []
            for key, (val, t) in toks.items():
                if (not t[3]) and t[2] == e:
                    continue
                if self.waited[e].get(key, 0) >= val:
                    continue
                self.waited[e][key] = val
                wl.append((self.sem(key), val))
            if wl:
                def emit(en, wl=wl):
                    for s, v in wl:
                        en.wait_ge(s, v)
                self.q[e].append(emit)
        keepw = {b: self.lastw[b] for b in list(keep) + ["OUTC"] if b in self.lastw}
        self.lastw = dict(keepw)
        self.readers = {}

    def finish(self, outbufs):
        wl = []
        for b in outbufs:
            t = self.lastw.get(b)
            if t is not None:
                wl.append((self.sem(t[0]), t[1]))

        def emit(e, wl=wl):
            for s, v in wl:
                e.wait_ge(s, v)
        self.q["sp"].append(emit)


def build_nc():
    nc = bass.Bass("TRN2", target_bir_lowering=False)
    P = Prog(nc)

    def din(name, shape, dt=F32):
        return nc.dram_tensor(name, list(shape), dt, kind="ExternalInput").ap()

    def dout(name, shape):
        return nc.dram_tensor(name, list(shape), F32, kind="ExternalOutput").ap()

    def dscr(name, shape, dt):
        if DBG_DUMP and name[2] == "0":
            return nc.dram_tensor(name, list(shape), dt, kind="ExternalOutput").ap()
        return nc.dram_tensor(name, list(shape), dt).ap()
    dbg_attn = nc.dram_tensor("dbg_attn", [128, 4, TC], BF16, kind="ExternalOutput").ap() if DBG_DUMP else None

    xT = din("xT", [D, SEQ])
    xsT = din("xsT", [D, NS])
    stT = din("stT", [2, D, NS, 15])
    st = din("st", [2, NS, 15, D])
    cch = [din("c128", [2, NS, 128, 2, 512]), din("c512", [2, NS, 512, 2, 512]), din("c2048", [2, NS, 2048, 2, 512])]
    gmix = din("gmix", [128, 4, 8])
    gmlp = din("gmlp", [128, 4, 8])
    pscale = din("pscale", [128, 2, 8])
    pool_w = din("pool_w", [2, 4, 256, 256])
    wqkv = din("wqkv", [2, D, 4608])
    wo = din("wo", [2, 512, D])
    wup = din("wup", [4, D, FF])
    wdn = din("wdn", [4, FF, D])
    qkn = din("qkn", [2, 3, 2, 128, 512])
    sintab = din("sintab", [SEQ + NS, 32])
    costab2 = din("costab2", [SEQ + NS, 64])
    nsintab = din("nsintab", [SEQ + NS, 32])
    masks_d = din("masks", [128, 256], BF16)
    invc_d = din("invc", [128, 8, 16])
    identb_d = din("identb", [128, 128], BF16)
    identf_d = din("identf", [128, 128])

    yT = dout("yT", [D, SEQ])
    ysT = dout("ysT", [D, NS])
    poolp = dout("poolp", [2, 15, D])
    pools = dout("pools", [2, NS, 15, D])
    kvp = [dout("kp128", [2, 128, 2, 512]), dout("kp512", [2, 512, 2, 512]), dout("kp2048", [2, 2048, 2, 512])]
    kvs = [dout("ks128", [2, NS, 128, 2, 512]), dout("ks512", [2, NS, 512, 2, 512]), dout("ks2048", [2, NS, 2048, 2, 512])]

    Ks = [[dscr("Ks%d%d" % (li, g), [SEQ + 16, 512], BF16) for g in range(3)] for li in range(2)]
    Vs = [[dscr("Vs%d%d" % (li, g), [SEQ + 16, 512], BF16) for g in range(3)] for li in range(2)]
    Qs = [[dscr("Qs%d%d" % (li, g), [SEQ + 16, 512], BF16) for g in range(3)] for li in range(2)]

    wup_b = dscr("wupb", [4, D, FF], BF16)
    wdn_b = dscr("wdnb", [4, FF, D], BF16)
    wqkv_b = dscr("wqkvb", [2, D, 4608], BF16)
    wo_b = dscr("wob", [2, 512, D], BF16)
    poolw_b = dscr("poolwb", [2, 4, 256, 256], BF16)

    def convert(src, dst, pat, kw, nsplit, key):
        sv = src.rearrange(pat, **kw)
        dv = dst.rearrange(pat, **kw)
        rows = sv.shape[0]
        step = rows // nsplit
        for i in range(nsplit):
            P.op("pool", lambda e, sv=sv, dv=dv, i=i, step=step: e.dma_start(out=dv[i * step:(i + 1) * step, :], in_=sv[i * step:(i + 1) * step, :]),
                 writes=[key], dma=True)

    for l in range(4):
        convert(wup[l], wup_b[l], "d (a f) -> (d a) f", dict(f=2048), 4, "cv_up")
        convert(wdn[l], wdn_b[l], "(r a) d -> r (a d)", dict(a=2), 4, "cv_dn")
    for li in range(2):
        convert(wqkv[li], wqkv_b[li], "d (a f) -> (d a) f", dict(f=1536), 4, "cv_qkv")
        convert(wo[li], wo_b[li], "r d -> r d", dict(), 1, "cv_o")
        convert(pool_w[li], poolw_b[li], "g c e -> (g c) e", dict(), 1, "cv_p")

    base = ((nc.sbuf_base + 63) // 64) * 64
    top = nc.sbuf_top
    state = {"off": base, "n": 0}

    def alloc(shape, dt):
        esz = 4 if dt == F32 else 2
        nb = esz
        for s in shape[1:]:
            nb *= s
        nb = ((nb + 63) // 64) * 64
        off = state["off"]
        assert off + nb <= top, ("SBUF overflow", off, nb, top)
        state["off"] = off + nb
        state["n"] += 1
        return nc.alloc_sbuf_tensor_at("t%d" % state["n"], list(shape), dt, offset=off)

    x = alloc([128, 8, TC], F32)
    h = alloc([128, 8, TC], BF16)
    ones_bf = alloc([128, 128], BF16)
    identb = alloc([128, 128], BF16)
    identf = alloc([128, 128], F32)
    masks = alloc([128, 256], BF16)
    invc = alloc([128, 8, 16], F32)
    g_mix = alloc([128, 4, 8], F32)
    g_mlp = alloc([128, 4, 8], F32)
    p_scale = alloc([128, 2, 8], F32)
    hprev = alloc([128, 2, 8, 15], F32)
    phase_base = state["off"]

    ps = [nc.alloc_psum_tensor("ps%d" % i, [128, 512], F32) for i in range(7)]
    psb = nc.alloc_psum_tensor("psb", [128, 1024], BF16)
    psrr = {"i": 0}

    def next_ps():
        i = psrr["i"] % 7
        psrr["i"] += 1
        return ps[i], ("ps", i)

    P.op("sp", lambda e: e.dma_start(out=identb[:], in_=identb_d), writes=["identb"], dma=True)
    P.op("sp", lambda e: e.dma_start(out=identf[:], in_=identf_d), writes=["identf"], dma=True)
    P.op("sp", lambda e: e.dma_start(out=masks[:], in_=masks_d), writes=["masks"], dma=True)
    P.op("sp", lambda e: e.dma_start(out=invc[:], in_=invc_d), writes=["invc"], dma=True)
    P.op("sp", lambda e: e.dma_start(out=g_mix[:], in_=gmix), writes=["g_mix"], dma=True)
    P.op("sp", lambda e: e.dma_start(out=g_mlp[:], in_=gmlp), writes=["g_mlp"], dma=True)
    P.op("sp", lambda e: e.dma_start(out=p_scale[:], in_=pscale), writes=["p_scale"], dma=True)
    P.op("dve", lambda e: e.memset(ones_bf[:], 1.0), writes=["ones"])
    for li_ in range(2):
        for g_ in range(3):
            for b_ in range(NS):
                P.op("sp", lambda e, g_=g_, li_=li_, b_=b_: e.dma_start(out=kvs[g_][li_, b_, 0:NBUF[g_] - 1, :, :], in_=cch[g_][li_, b_, 1:NBUF[g_], :, :]),
                     writes=["OUTC"], dma=True, nowaw=True)
    CONST = ["identb", "identf", "masks", "invc", "g_mix", "g_mlp", "p_scale", "ones"]

    def tiles_of(T):
        if T >= 512:
            return [(i * 512, 512) for i in range(T // 512)]
        return [(0, T)]

    def rmsnorm(T, gt, gkey, l, out_fn, sq, rs, extra_w):
        for ti, (c0, n) in enumerate(tiles_of(T)):
            for k in range(8):
                P.op("act", lambda e, k=k, c0=c0, n=n: e.activation(out=sq[:, k, 0:n], in_=x[:, k, c0:c0 + n], func=AF.Square),
                     reads=[("x", k, ti)], writes=[("sq", k)])
            pt, pk = next_ps()
            for k in range(8):
                P.op("pe", lambda e, k=k, n=n, pt=pt: e.matmul(pt[:, 0:n], lhsT=ones_bf[:], rhs=sq[:, k, 0:n], start=(k == 0), stop=(k == 7)),
                     reads=[("sq", k), "ones"], writes=[pk], signal=(k == 7))
            P.op("act", lambda e, n=n, pt=pt: e.activation(out=rs[:, 0:n], in_=pt[:, 0:n], func=AF.Sqrt, bias=EPS, scale=1.0 / D),
                 reads=[pk], writes=["rs"])
            P.op("dve", lambda e, n=n: e.reciprocal(out=rs[:, 0:n], in_=rs[:, 0:n]), reads=["rs"], writes=["rs"])
            for k in range(8):
                o = out_fn(k, c0, n)
                P.op("dve", lambda e, k=k, c0=c0, n=n, o=o: e.scalar_tensor_tensor(out=o, in0=x[:, k, c0:c0 + n], scalar=gt[:, l, k:k + 1], in1=rs[:, 0:n], op0=ALU.mult, op1=ALU.mult),
                     reads=[("x", k, ti), "rs", gkey], writes=[extra_w(k, ti)])

    def mlp(T, l):
        state["off"] = phase_base
        P.barrier(keep=CONST)
        sq = alloc([128, 8, 512], BF16)
        rs = alloc([128, 512], F32)
        rmsnorm(T, g_mlp, "g_mlp", l, lambda k, c0, n: h[:, k, c0:c0 + n], sq, rs, lambda k, ti: ("h", k, ti))
        hid = [alloc([128, 4, T], BF16) for _ in range(2)]
        NW = 3
        wu = [alloc([128, 8, 512], BF16) for _ in range(NW)]
        wd = [alloc([128, 4, D], BF16) for _ in range(NW)]
        NRL = 4
        rl = [alloc([128, 512], F32) for _ in range(NRL)]
        tl = tiles_of(T)
        rlc = {"i": 0}

        def load_w(fb):
            s = fb % NW
            P.op("sp", lambda e: e.dma_start(out=wu[s][:], in_=wup_b[l].rearrange("(k p) f -> p k f", p=128)[:, :, fb * 512:(fb + 1) * 512]),
                 writes=[("wu", s)], dma=True)
            P.op("sp", lambda e: e.dma_start(out=wd[s][:], in_=wdn_b[l, fb * 512:(fb + 1) * 512, :].rearrange("(k p) d -> p k d", p=128)),
                 writes=[("wd", s)], dma=True)

        def up(fb):
            s = fb % NW
            hs = fb % 2
            for fc in range(4):
                for ti, (c0, n) in enumerate(tl):
                    pt, pk = next_ps()
                    for k in range(8):
                        P.op("pe", lambda e, k=k, c0=c0, n=n, pt=pt, fc=fc: e.matmul(pt[:, 0:n], lhsT=wu[s][:, k, fc * 128:(fc + 1) * 128], rhs=h[:, k, c0:c0 + n], start=(k == 0), stop=(k == 7)),
                             reads=[("wu", s), ("h", k, ti)], writes=[pk], signal=(k == 7))
                    r = rlc["i"] % NRL
                    rlc["i"] += 1
                    P.op("act", lambda e, n=n, pt=pt, r=r: e.activation(out=rl[r][:, 0:n], in_=pt[:, 0:n], func=AF.Relu),
                         reads=[pk], writes=[("rl", r)])
                    P.op("pool", lambda e, n=n, c0=c0, r=r, fc=fc: e.tensor_tensor(out=hid[hs][:, fc, c0:c0 + n], in0=rl[r][:, 0:n], in1=rl[r][:, 0:n], op=ALU.mult),
                         reads=[("rl", r)], writes=[("hid", hs, fc, ti)])

        def down(fb):
            s = fb % NW
            hs = fb % 2
            for dc in range(8):
                for ti, (c0, n) in enumerate(tl):
                    pt, pk = next_ps()
                    for fc in range(4):
                        P.op("pe", lambda e, fc=fc, c0=c0, n=n, pt=pt, dc=dc: e.matmul(pt[:, 0:n], lhsT=wd[s][:, fc, dc * 128:(dc + 1) * 128], rhs=hid[hs][:, fc, c0:c0 + n], start=(fc == 0), stop=(fc == 3)),
                             reads=[("wd", s), ("hid", hs, fc, ti)], writes=[pk], signal=(fc == 3))
                    P.op("dve", lambda e, c0=c0, n=n, pt=pt, dc=dc: e.tensor_tensor(out=x[:, dc, c0:c0 + n], in0=pt[:, 0:n], in1=x[:, dc, c0:c0 + n], op=ALU.add),
                         reads=[pk, ("x", dc, ti)], writes=[("x", dc, ti)])

        load_w(0)
        load_w(1)
        up(0)
        for fb in range(8):
            if fb + 2 < 8:
                load_w(fb + 2)
            if fb + 1 < 8:
                up(fb + 1)
            down(fb)

    def pool_layer(T, l, ci, sample):
        li = l // 2
        state["off"] = phase_base
        P.barrier(keep=CONST + ["hprev0", "hprev1"])
        sq = alloc([128, 8, 512], BF16)
        rs = alloc([128, 512], F32)
        wp = alloc([128, 4, 2, 256], BF16)
        for gi in range(4):
            P.op("sp", lambda e, gi=gi: e.dma_start(out=wp[:, gi, :, :], in_=poolw_b[li, gi].rearrange("(c p) e -> p c e", p=128)), writes=["wp"], dma=True)
        if not sample:
            E = alloc([128, 8, 16 + 512], F32)
            W1 = alloc([128, 2, 16 + 512], F32)
            W2 = alloc([128, 2, 16 + 512], F32)
            tmp16 = alloc([128, 2, 16], F32)
            tl = tiles_of(T)
            for ti, (c0, n) in enumerate(tl):
                if ti == 0:
                    if ci == 0:
                        P.op("dve", lambda e: e.memset(E[:, :, 0:16], 0.0), writes=["Eh"])
                    else:
                        P.op("dve", lambda e: e.tensor_copy(out=E[:, :, 1:16], in_=hprev[:, li, :, :]), reads=["hprev%d" % li], writes=["Eh"])
                else:
                    P.op("dve", lambda e: e.tensor_copy(out=E[:, :, 1:16], in_=E[:, :, 513:528]), reads=["E"], writes=["Eh"])
                for k in range(8):
                    P.op("act", lambda e, k=k, c0=c0, n=n: e.activation(out=sq[:, k, 0:n], in_=x[:, k, c0:c0 + n], func=AF.Square),
                         reads=[("x", k, ti)], writes=[("sq", k)])
                pt, pk = next_ps()
                for k in range(8):
                    P.op("pe", lambda e, k=k, n=n, pt=pt: e.matmul(pt[:, 0:n], lhsT=ones_bf[:], rhs=sq[:, k, 0:n], start=(k == 0), stop=(k == 7)),
                         reads=[("sq", k), "ones"], writes=[pk], signal=(k == 7))
                P.op("act", lambda e, n=n, pt=pt: e.activation(out=rs[:, 0:n], in_=pt[:, 0:n], func=AF.Sqrt, bias=EPS, scale=1.0 / D),
                     reads=[pk], writes=["rs"])
                P.op("dve", lambda e, n=n: e.reciprocal(out=rs[:, 0:n], in_=rs[:, 0:n]), reads=["rs"], writes=["rs"])
                for k in range(8):
                    P.op("dve", lambda e, k=k, c0=c0, n=n: e.scalar_tensor_tensor(out=E[:, k, 16:16 + n], in0=x[:, k, c0:c0 + n], scalar=g_mix[:, l, k:k + 1], in1=rs[:, 0:n], op0=ALU.mult, op1=ALU.mult),
                         reads=[("x", k, ti), "rs", "g_mix", "Eh"], writes=["E"])
                L = 16 + n
                for gi in range(4):
                    w = WIN[gi]
                    k0 = 2 * gi
                    P.op("pool", lambda e, k0=k0, L=L: e.tensor_tensor(out=W1[:, :, 1:L], in0=E[:, k0:k0 + 2, 1:L], in1=E[:, k0:k0 + 2, 0:L - 1], op=ALU.add),
                         reads=["E", "Eh"], writes=["W1"])
                    cur = W1
                    ck = "W1"
                    if w >= 4:
                        P.op("pool", lambda e, L=L: e.tensor_tensor(out=W2[:, :, 3:L], in0=W1[:, :, 3:L], in1=W1[:, :, 1:L - 2], op=ALU.add),
                             reads=["W1"], writes=["W2"])
                        cur, ck = W2, "W2"
                    if w >= 8:
                        P.op("pool", lambda e, L=L: e.tensor_tensor(out=W1[:, :, 7:L], in0=W2[:, :, 7:L], in1=W2[:, :, 3:L - 4], op=ALU.add),
                             reads=["W2"], writes=["W1"])
                        cur, ck = W1, "W1"
                    if w >= 16:
                        P.op("pool", lambda e, L=L: e.tensor_tensor(out=W2[:, :, 15:L], in0=W1[:, :, 15:L], in1=W1[:, :, 7:L - 8], op=ALU.add),
                             reads=["W1"], writes=["W2"])
                        cur, ck = W2, "W2"
                    P.op("dve", lambda e, cur=cur, k0=k0, c0=c0, n=n, w=w: e.scalar_tensor_tensor(out=h[:, k0:k0 + 2, c0:c0 + n], in0=cur[:, :, 16:16 + n], scalar=1.0 / w, in1=E[:, k0:k0 + 2, 16:16 + n], op0=ALU.mult, op1=ALU.subtract),
                         reads=[ck, "E"], writes=[("h", k0, ti), ("h", k0 + 1, ti)])
                    if ci == 0 and ti == 0:
                        P.op("dve", lambda e, cur=cur, k0=k0: e.tensor_tensor(out=tmp16[:], in0=cur[:, :, 16:32], in1=invc[:, k0:k0 + 2, :], op=ALU.mult),
                             reads=[ck, "invc"], writes=["tmp16"])
                        P.op("dve", lambda e, k0=k0: e.tensor_tensor(out=h[:, k0:k0 + 2, 0:16], in0=tmp16[:], in1=E[:, k0:k0 + 2, 16:32], op=ALU.subtract),
                             reads=["tmp16", "E"], writes=[("h", k0, ti), ("h", k0 + 1, ti)])
                if ti == len(tl) - 1:
                    P.op("dve", lambda e, n=n: e.tensor_copy(out=hprev[:, li, :, :], in_=E[:, :, 16 + n - 15:16 + n]), reads=["E"], writes=["hprev%d" % li])
                    if ci == 3:
                        rows = alloc([15, D], F32)
                        for hf in range(2):
                            pt, pk = next_ps()
                            for k in range(4):
                                P.op("pe", lambda e, k=k, n=n, pt=pt, hf=hf: e.transpose(pt[0:15, k * 128:(k + 1) * 128], E[:, hf * 4 + k, 16 + n - 15:16 + n], identf[:]),
                                     reads=["E", "identf"], writes=[pk], signal=(k == 3))
                            P.op("dve", lambda e, pt=pt, hf=hf: e.tensor_copy(out=rows[:, hf * 512:(hf + 1) * 512], in_=pt[0:15, :]), reads=[pk], writes=["prow"])
                        P.op("sp", lambda e: e.dma_start(out=poolp[li], in_=rows[:]), reads=["prow"], writes=["OUT"], dma=True)
                pool_matmul(wp, T, li, [(ti, c0, n)])
        else:
            Es = alloc([128, 8, NS, 16], F32)
            ssum = alloc([128, 2, NS], F32)
            rows = alloc([NS, D], F32)
            for k in range(8):
                P.op("sp", lambda e, k=k: e.dma_start(out=Es[:, k, :, 0:15], in_=stT[li, k * 128:(k + 1) * 128, :, :]), writes=["Es"], dma=True)
            rmsnorm(T, g_mix, "g_mix", l, lambda k, c0, n: Es[:, k, :, 15], sq, rs, lambda k, ti: "Es")
            for gi in range(4):
                w = WIN[gi]
                k0 = 2 * gi
                P.op("dve", lambda e, k0=k0, w=w: e.tensor_reduce(out=ssum[:], in_=Es[:, k0:k0 + 2, :, 16 - w:16], axis=AX.X, op=ALU.add),
                     reads=["Es"], writes=["ssum"])
                P.op("dve", lambda e, k0=k0, w=w: e.scalar_tensor_tensor(out=h[:, k0:k0 + 2, 0:NS], in0=ssum[:], scalar=1.0 / w, in1=Es[:, k0:k0 + 2, :, 15], op0=ALU.mult, op1=ALU.subtract),
                     reads=["ssum", "Es"], writes=[("h", k0, 0), ("h", k0 + 1, 0)])
            P.op("sp", lambda e: e.dma_start(out=pools[li, :, 0:14, :], in_=st[li, :, 1:15, :]), writes=["OUT"], dma=True)
            for hf in range(2):
                pt, pk = next_ps()
                for k in range(4):
                    P.op("pe", lambda e, k=k, pt=pt, hf=hf: e.transpose(pt[0:NS, k * 128:(k + 1) * 128], Es[:, hf * 4 + k, :, 15], identf[:]),
                         reads=["Es", "identf"], writes=[pk], signal=(k == 3))
                P.op("dve", lambda e, pt=pt, hf=hf: e.tensor_copy(out=rows[:, hf * 512:(hf + 1) * 512], in_=pt[0:NS, :]), reads=[pk], writes=["prow"])
            P.op("sp", lambda e: e.dma_start(out=pools[li, :, 14, :], in_=rows[:]), reads=["prow"], writes=["OUT"], dma=True)
            pool_matmul(wp, T, li, [(0, 0, NS)])

    def pool_matmul(wp, T, li, tl):
        for ti, c0, n in tl:
            for gi in range(4):
                for ec in range(2):
                    pt, pk = next_ps()
                    for cc in range(2):
                        P.op("pe", lambda e, gi=gi, ec=ec, cc=cc, c0=c0, n=n, pt=pt: e.matmul(pt[:, 0:n], lhsT=wp[:, gi, cc, ec * 128:(ec + 1) * 128], rhs=h[:, 2 * gi + cc, c0:c0 + n], start=(cc == 0), stop=(cc == 1)),
                             reads=["wp", ("h", 2 * gi + cc, ti)], writes=[pk], signal=(cc == 1))
                    kk = 2 * gi + ec
                    P.op("dve", lambda e, kk=kk, c0=c0, n=n, pt=pt: e.scalar_tensor_tensor(out=x[:, kk, c0:c0 + n], in0=pt[:, 0:n], scalar=p_scale[:, li, kk:kk + 1], in1=x[:, kk, c0:c0 + n], op0=ALU.mult, op1=ALU.add),
                         reads=[pk, ("x", kk, ti), "p_scale"], writes=[("x", kk, ti)])

    def attn_layer(T, l, ci, sample):
        li = l // 2
        state["off"] = phase_base
        P.barrier(keep=CONST)
        sq = alloc([128, 8, 512], BF16)
        rs = alloc([128, 512], F32)
        rmsnorm(T, g_mix, "g_mix", l, lambda k, c0, n: h[:, k, c0:c0 + n], sq, rs, lambda k, ti: ("h", k, ti))
        row0 = SEQ if sample else ci * TC
        nblk = 1 if sample else T // 128
        npb = NS if sample else 128
        wq = [alloc([128, 8, 512], BF16) for _ in range(2)]
        gn = [alloc([128, 512], F32) for _ in range(2)]
        cs = alloc([128, 16, 32], F32)
        sn = alloc([128, 16, 32], F32)
        qf = [alloc([128, 512], F32) for _ in range(4)]
        qn = [alloc([128, 512], F32) for _ in range(4)]
        sqf2 = [alloc([128, 512], F32) for _ in range(4)]
        ssh2 = [alloc([128, 8], F32) for _ in range(4)]
        t1a2 = [alloc([128, 8, 32], F32) for _ in range(4)]
        t2a2 = [alloc([128, 8, 32], F32) for _ in range(4)]
        t1b2 = [alloc([128, 8, 32], F32) for _ in range(4)]
        t2b2 = [alloc([128, 8, 32], F32) for _ in range(4)]
        sqf = sqf2[0]
        qb16 = [alloc([128, 512], BF16) for _ in range(4)]
        ssh = alloc([128, 8], F32)
        t1 = alloc([128, 8, 32], F32)
        t2 = alloc([128, 8, 32], F32)
        cs2 = alloc([128, 16, 64], F32)
        nsn = alloc([128, 16, 32], F32)
        t2f = [alloc([128, 512], F32) for _ in range(4)]
        P.op("sp", lambda e: e.dma_start(out=cs2[0:npb, 0:nblk, :], in_=costab2[row0:row0 + npb * nblk, :].rearrange("(b p) c -> p b c", p=npb)), writes=["cs2"], dma=True)
        P.op("sp", lambda e: e.dma_start(out=nsn[0:npb, 0:nblk, :], in_=nsintab[row0:row0 + npb * nblk, :].rearrange("(b p) c -> p b c", p=npb)), writes=["nsn"], dma=True)
        P.op("sp", lambda e: e.dma_start(out=sn[0:npb, 0:nblk, :], in_=sintab[row0:row0 + npb * nblk, :].rearrange("(b p) c -> p b c", p=npb)), writes=["sn"], dma=True)
        NBF = 4
        ptmap = {}

        def S1(tb, s, part, g):
            pt, pk = next_ps()
            for k in range(8):
                P.op("pe", lambda e, k=k: e.matmul(pt[0:npb, :], lhsT=h[:, k, tb * npb:(tb + 1) * npb], rhs=wq[s][:, k, :], start=(k == 0), stop=(k == 7)),
                     reads=[("wq", s), ("h", k, tb // 4)], writes=[pk], signal=(k == 7))
            r = tb % NBF
            of = qf[r]
            if part == 2:
                P.op("act", lambda e: e.activation(out=of[0:npb, :], in_=pt[0:npb, :], func=AF.Copy), reads=[pk], writes=[("qf", r)])
                return
            sqf_, ssh_, qq = sqf2[r], ssh2[r], qn[r]
            ptmap[tb] = (pt, pk)
            P.op("act", lambda e: e.activation(out=sqf_[0:npb, :], in_=pt[0:npb, :], func=AF.Square), reads=[pk], writes=[("sqf", r)])
            P.op("dve", lambda e: e.tensor_reduce(out=ssh_[0:npb, :], in_=sqf_[0:npb, :].rearrange("p (h d) -> p h d", d=64), axis=AX.X, op=ALU.add), reads=[("sqf", r)], writes=[("ssh", r)])

        def S1b(tb, s, part, g):
            if part == 2:
                return
            r = tb % NBF
            ssh_, qq = ssh2[r], qn[r]
            pt, pk = ptmap[tb]
            P.op("act", lambda e: e.activation(out=ssh_[0:npb, :], in_=ssh_[0:npb, :], func=AF.Sqrt, bias=EPS, scale=1.0 / 64), reads=[("ssh", r)], writes=[("ssh", r)])
            P.op("dve", lambda e: e.reciprocal(out=ssh_[0:npb, :], in_=ssh_[0:npb, :]), reads=[("ssh", r)], writes=[("ssh", r)])
            for hd in range(8):
                P.op("act", lambda e, hd=hd: e.mul(out=qq[0:npb, hd * 64:(hd + 1) * 64], in_=pt[0:npb, hd * 64:(hd + 1) * 64], mul=ssh_[0:npb, hd:hd + 1]),
                     reads=[pk, ("ssh", r)], writes=[("qn", r)], signal=(hd == 7))
            P.op("pool", lambda e: e.tensor_tensor(out=qq[0:npb, :], in0=qq[0:npb, :], in1=gn[s][0:npb, :], op=ALU.mult),
                 reads=[("qn", r), ("gn", s)], writes=[("qn", r)])

        def S2(tb, s, part, g):
            if part == 2:
                return
            r = tb % NBF
            qq = qn[r]
            t1f, t2f_ = sqf2[r], t2f[r]
            q3 = qq[0:npb, :].rearrange("p (h d) -> p h d", d=64)
            t13 = t1f[0:npb, :].rearrange("p (h d) -> p h d", d=64)
            t23 = t2f_[0:npb, :].rearrange("p (h d) -> p h d", d=64)
            cb = ap_bcast_mid(cs2[0:npb, tb, :], 8)
            sb = ap_bcast_mid(sn[0:npb, tb, :], 8)
            nsb = ap_bcast_mid(nsn[0:npb, tb, :], 8)
            P.op("dve", lambda e: e.tensor_tensor(out=t13, in0=q3, in1=cb, op=ALU.mult), reads=[("qn", r), "cs2"], writes=[("sqf", r)])
            P.op("pool", lambda e: e.tensor_tensor(out=t23[:, :, 0:32], in0=q3[:, :, 32:64], in1=nsb, op=ALU.mult), reads=[("qn", r), "nsn"], writes=[("t2f", r)])
            P.op("pool", lambda e: e.tensor_tensor(out=t23[:, :, 32:64], in0=q3[:, :, 0:32], in1=sb, op=ALU.mult), reads=[("qn", r), "sn"], writes=[("t2f", r)])

        def S3(tb, s, part, g):
            r = tb % NBF
            of = qf[r]
            if part < 2:
                t1f, t2f_ = sqf2[r], t2f[r]
                P.op("dve", lambda e: e.tensor_tensor(out=of[0:npb, :], in0=t1f[0:npb, :], in1=t2f_[0:npb, :], op=ALU.add), reads=[("sqf", r), ("t2f", r)], writes=[("qf", r)])
            scr = [Qs, Ks, Vs][part][li][g]
            r0 = row0 + tb * npb
            ob = qb16[r]
            P.op("act", lambda e: e.activation(out=ob[0:npb, :], in_=of[0:npb, :], func=AF.Copy),
                 reads=[("qf", r)], writes=[("qb16", r)])
            P.op("sp", lambda e: e.dma_start(out=scr[r0:r0 + npb, :], in_=ob[0:npb, :]),
                 reads=[("qb16", r)], writes=[("scr", part, g)], dma=True)
            if part >= 1:
                nb = NBUF[g]
                if sample:
                    P.op("sp", lambda e: e.dma_start(out=kvs[g][li, :, nb - 1, part - 1, :], in_=of[0:NS, :]),
                         reads=[("qf", r)], writes=["OUT"], dma=True)
                else:
                    pos0 = ci * TC + tb * 128
                    if pos0 >= SEQ - nb:
                        rr = pos0 - (SEQ - nb)
                        P.op("sp", lambda e: e.dma_start(out=kvp[g][li, rr:rr + 128, part - 1, :], in_=of[:, :]),
                             reads=[("qf", r)], writes=["OUT"], dma=True)

        combo = 0
        for g in range(3):
            for part in range(3):
                s = combo % 2
                combo += 1
                col0 = part * 1536 + g * 512
                P.op("sp", lambda e, s=s, col0=col0: e.dma_start(out=wq[s][:], in_=wqkv_b[li].rearrange("(k p) f -> p k f", p=128)[:, :, col0:col0 + 512]),
                     writes=[("wq", s)], dma=True)
                if part < 2:
                    P.op("sp", lambda e, s=s, g=g, part=part: e.dma_start(out=gn[s][:], in_=qkn[li, g, part]), writes=[("gn", s)], dma=True)
                for t in range(nblk + 3):
                    if 0 <= t - 3 < nblk:
                        S3(t - 3, s, part, g)
                    if 0 <= t - 2 < nblk:
                        S2(t - 2, s, part, g)
                    if 0 <= t - 1 < nblk:
                        S1b(t - 1, s, part, g)
                    if t < nblk:
                        S1(t, s, part, g)
        state["off"] = phase_base
        P.barrier(keep=CONST)
        accOL = alloc([128, 2, T], F32)
        accO = accOL[:, 0, :]
        accL = accOL[:, 1, :]
        attnT = alloc([128, 4, T], BF16)
        onesc = ones_bf
        NB1 = 32
        qrows2 = [alloc([128, 16, 128], BF16) for _ in range(2)]
        krows2 = [alloc([128, NB1, 128], BF16) for _ in range(2)]
        vrows2 = [alloc([128, NB1, 128], BF16) for _ in range(2)]
        qT2 = [alloc([128, 16 * 128], BF16) for _ in range(2)]
        kT2 = [alloc([128, NB1 * 128], BF16) for _ in range(2)]
        qrows, krows, vrows, qT, kT = qrows2[0], krows2[0], vrows2[0], qT2[0], kT2[0]
        NPT = 6
        pT = [alloc([128, 256], BF16) for _ in range(NPT)]
        knew = alloc([NS, 128], BF16)
        vnew = alloc([1, NS, 128], BF16)
        scale = 64 ** -0.5
        clsn = {"i": 0, "it": 0}

        def acc_keys(nm, g, hh, qb):
            if g == 0:
                return [(nm, hh, qb // 4)]
            if g == 1:
                return [(nm, hh, qb)]
            return [(nm, hh, t) for t in range(4)]

        def class_setup(pc, g, cs_):
            Dg = DIL[g]
            nqb = (T // Dg) // 128
            first = ci * TC
            halo = (ci > 0)
            hoff = 1 if halo else 0
            nkc = nqb + hoff
            qr, kr, vr, qt, kt = qrows2[cs_], krows2[cs_], vrows2[cs_], qT2[cs_], kT2[cs_]
            if Dg == 1:
                kstart = first - (128 if halo else 0)
                P.op("sp", lambda e: e.dma_start(out=qr[:, 0:nqb, :], in_=Qs[li][g][first:first + 128 * nqb, pc].rearrange("(b p) c -> p b c", p=128)),
                     reads=[("scr", 0, g)], writes=[("qrows", cs_)], dma=True)
                P.op("sp", lambda e: e.dma_start(out=kr[:, 0:nkc, :], in_=Ks[li][g][kstart:kstart + 128 * nkc, pc].rearrange("(b p) c -> p b c", p=128)),
                     reads=[("scr", 1, g)], writes=[("krows", cs_)], dma=True)
                P.op("sp", lambda e: e.dma_start(out=vr[:, 0:nkc, :], in_=Vs[li][g][kstart:kstart + 128 * nkc, pc].rearrange("(b p) c -> p b c", p=128)),
                     reads=[("scr", 2, g)], writes=[("vrows", cs_)], dma=True)
            else:
                span = 128 * Dg
                for bb in range(nqb):
                    r0 = first + span * bb
                    P.op("sp", lambda e, bb=bb, r0=r0: e.dma_start(out=qr[:, 0:Dg * nqb, :].rearrange("p (r b) c -> p b r c", b=nqb)[:, bb, :, :], in_=Qs[li][g][r0:r0 + span, pc].rearrange("(p r) c -> p r c", r=Dg)),
                         reads=[("scr", 0, g)], writes=[("qrows", cs_)], dma=True)
                for bb in range(-hoff, nqb):
                    r0 = first + span * bb
                    for (scr_, t_, key_, part_) in ((Ks, kr, "krows", 1), (Vs, vr, "vrows", 2)):
                        P.op("sp", lambda e, bb=bb, r0=r0, scr_=scr_, t_=t_: e.dma_start(out=t_[:, 0:Dg * nkc, :].rearrange("p (r k) c -> p k r c", k=nkc)[:, bb + hoff, :, :], in_=scr_[li][g][r0:r0 + span, pc].rearrange("(p r) c -> p r c", r=Dg)),
                             reads=[("scr", part_, g)], writes=[(key_, cs_)], dma=True)
            for (src, skey, dst, dkey, nbk) in ((qr, ("qrows", cs_), qt, ("qT", cs_), Dg * nqb), (kr, ("krows", cs_), kt, ("kT", cs_), Dg * nkc)):
                for b0 in range(0, nbk, 8):
                    nn = min(8, nbk - b0)
                    for j in range(nn):
                        P.op("pe", lambda e, src=src, b0=b0, j=j: e.transpose(psb[:, j * 128:(j + 1) * 128], src[:, b0 + j, :], identb[:]),
                             reads=[skey, "identb"], writes=["psb"], signal=(j == nn - 1))
                    P.op("act", lambda e, dst=dst, b0=b0, nn=nn: e.activation(out=dst[:, b0 * 128:(b0 + nn) * 128], in_=psb[:, 0:nn * 128], func=AF.Copy),
                         reads=["psb"], writes=[dkey])
            items = []
            for r in range(Dg):
                for qb in range(nqb):
                    kbs = []
                    if halo or qb > 0:
                        kbs.append((r * nkc + qb - 1 + hoff, 0))
                    kbs.append((r * nkc + qb + hoff, 1))
                    for hh in range(2):
                        items.append(dict(g=g, rcl=r, cs=cs_, qb=qb, qi=r * nqb + qb, hh=hh, kbs=kbs, Dg=Dg))
            return items

        def stage_a(it):
            cs_, qb, hh, kbs = it["cs"], it["qi"], it["hh"], it["kbs"]
            hp = slice(hh * 64, (hh + 1) * 64)
            qt, kt = qT2[cs_], kT2[cs_]
            pt, pk = next_ps()
            for (kb, mi) in kbs:
                P.op("pe", lambda e, kb=kb, mi=mi: e.matmul(pt[:, mi * 128:(mi + 1) * 128], lhsT=kt[hp, kb * 128:(kb + 1) * 128], rhs=qt[hp, qb * 128:(qb + 1) * 128], start=True, stop=True),
                     reads=[("kT", cs_), ("qT", cs_)], writes=[pk], signal=(mi == 1))
            c_lo = 0 if len(kbs) == 2 else 128
            pi = clsn["it"] % NPT
            clsn["it"] += 1
            pp = pT[pi]
            it["pi"], it["pp"] = pi, pp
            P.op("act", lambda e: e.activation(out=pp[:, c_lo:256], in_=pt[:, c_lo:256], func=AF.Exp, scale=scale),
                 reads=[pk], writes=[("pT", pi)])
            P.op("pool", lambda e: e.tensor_tensor(out=pp[:, c_lo:256], in0=pp[:, c_lo:256], in1=masks[:, c_lo:256], op=ALU.mult),
                 reads=[("pT", pi), "masks"], writes=[("pT", pi)])

        def stage_b(it):
            cs_, qb, hh, kbs, g, rcl, Dg = it["cs"], it["qb"], it["hh"], it["kbs"], it["g"], it["rcl"], it["Dg"]
            pi, pp = it["pi"], it["pp"]
            hp = slice(hh * 64, (hh + 1) * 64)
            vr = vrows2[cs_]
            po, pok = next_ps()
            nk = len(kbs)
            for i, (kb, mi) in enumerate(kbs):
                P.op("pe", lambda e, kb=kb, mi=mi, i=i: e.matmul(po[hp, 0:128], lhsT=vr[:, kb, hp], rhs=pp[:, mi * 128:(mi + 1) * 128], start=(i == 0), stop=(i == nk - 1)),
                     reads=[("vrows", cs_), ("pT", pi)], writes=[pok], signal=False)
            for i, (kb, mi) in enumerate(kbs):
                P.op("pe", lambda e, mi=mi, i=i: e.matmul(po[hp, 128:256], lhsT=onesc[:, hp], rhs=pp[:, mi * 128:(mi + 1) * 128], start=(i == 0), stop=(i == nk - 1)),
                     reads=["ones", ("pT", pi)], writes=[pok], signal=(i == nk - 1))
            aOL = accOL[hp, :, :].rearrange("p a (n d) -> p a d n", d=Dg)[:, :, rcl, qb * 128:(qb + 1) * 128]
            ks_ = acc_keys("acc", g, hh, qb)
            P.op("dve", lambda e: e.tensor_tensor(out=aOL, in0=po[hp, 0:256].rearrange("p (a n) -> p a n", a=2), in1=aOL, op=ALU.add),
                 reads=[pok] + ks_, writes=ks_)

        ALLACC = [("acc", hh, t) for hh in range(2) for t in range(4)]
        for pr in range(4):
            pc = slice(pr * 128, (pr + 1) * 128)
            P.op("dve", lambda e: e.memset(accOL[:], 0.0), writes=ALLACC)
            if not sample:
                classes = [0, 1, 2]
                cur = class_setup(pc, classes[0], clsn["i"] % 2)
                clsn["i"] += 1
                for cidx in range(len(classes)):
                    nxt = None
                    if cidx + 1 < len(classes):
                        nxt = class_setup(pc, classes[cidx + 1], clsn["i"] % 2)
                        clsn["i"] += 1
                    LOOK = 3
                    for i in range(min(LOOK, len(cur))):
                        stage_a(cur[i])
                    for i in range(len(cur)):
                        stage_b(cur[i])
                        if i + LOOK < len(cur):
                            stage_a(cur[i + LOOK])
                    cur = nxt
            for g in range(3):
                Dg = DIL[g]
                if not sample:
                    pass
                else:
                    P.op("sp", lambda e, pc=pc, g=g: e.dma_start(out=qrows[0:NS, 0, :], in_=Qs[li][g][SEQ:SEQ + NS, pc]), reads=[("scr", 0, g)], writes=["qrows"], dma=True)
                    P.op("sp", lambda e, pc=pc, g=g: e.dma_start(out=knew[:, :], in_=Ks[li][g][SEQ:SEQ + NS, pc]), reads=[("scr", 1, g)], writes=["knew"], dma=True)
                    P.op("sp", lambda e, pc=pc, g=g: e.dma_start(out=vnew[0:1, :, :], in_=Vs[li][g][SEQ:SEQ + NS, pc].rearrange("(o b) c -> o b c", o=1)), reads=[("scr", 2, g)], writes=["vnew"], dma=True)
                    P.op("pe", lambda e: e.transpose(psb[:, 0:NS], qrows[0:NS, 0, :], identb[0:NS, 0:NS]), reads=["qrows", "identb"], writes=["psb"], signal=False)
                    P.op("pe", lambda e: e.transpose(psb[:, 128:128 + NS], knew[:, :], identb[0:NS, 0:NS]), reads=["knew", "identb"], writes=["psb"])
                    P.op("act", lambda e: e.activation(out=qT[:, 0:NS], in_=psb[:, 0:NS], func=AF.Copy), reads=["psb"], writes=["qT"])
                    P.op("act", lambda e: e.activation(out=qT[:, 128:128 + NS], in_=psb[:, 128:128 + NS], func=AF.Copy), reads=["psb"], writes=["qT"])
                    for b in range(NS):
                        P.op("pool", lambda e, pc=pc, g=g, b=b, Dg=Dg: e.dma_start(out=krows[:, b, :], in_=cch[g][li, b].rearrange("(n d) t c -> d n t c", d=Dg)[0, :, 0, pc]), writes=["krows"], dma=True)
                        P.op("pool", lambda e, pc=pc, g=g, b=b, Dg=Dg: e.dma_start(out=vrows[:, b, :], in_=cch[g][li, b].rearrange("(n d) t c -> d n t c", d=Dg)[0, :, 1, pc]), writes=["vrows"], dma=True)
                    for b in range(NS):
                        P.op("pe", lambda e, b=b: e.transpose(psb[:, b * 128:(b + 1) * 128], krows[:, b, :], identb[:]), reads=["krows", "identb"], writes=["psb"], signal=(b == NS - 1))
                    P.op("act", lambda e: e.activation(out=kT[:, 0:NS * 128], in_=psb[:, 0:NS * 128], func=AF.Copy), reads=["psb"], writes=["kT"])
                    for b in range(NS):
                        for hh in range(2):
                            hp = slice(hh * 64, (hh + 1) * 64)
                            pt, pk = next_ps()
                            P.op("pe", lambda e, b=b, hp=hp, pt=pt: e.matmul(pt[:, 0:1], lhsT=kT[hp, b * 128:(b + 1) * 128], rhs=qT[hp, b:b + 1], start=True, stop=True),
                                 reads=["kT", "qT"], writes=[pk], signal=False)
                            P.op("pe", lambda e, b=b, hp=hp, pt=pt: e.matmul(pt[0:1, 2:3], lhsT=qT[hp, 128 + b:129 + b], rhs=qT[hp, b:b + 1], start=True, stop=True),
                                 reads=["qT"], writes=[pk])
                            pi = (b * 2 + hh) % 2
                            pp = pT[pi]
                            P.op("act", lambda e, pt=pt, pp=pp: e.activation(out=pp[:, 0:1], in_=pt[:, 0:1], func=AF.Exp, scale=scale), reads=[pk], writes=[("pT", pi)])
                            P.op("act", lambda e, pt=pt, pp=pp: e.activation(out=pp[0:1, 2:3], in_=pt[0:1, 2:3], func=AF.Exp, scale=scale), reads=[pk], writes=[("pT", pi)])
                            po, pok = next_ps()
                            P.op("pe", lambda e, b=b, hp=hp, po=po, pp=pp: e.matmul(po[hp, 0:1], lhsT=vrows[:, b, hp], rhs=pp[:, 0:1], start=True, stop=False),
                                 reads=["vrows", ("pT", pi)], writes=[pok], signal=False)
                            P.op("pe", lambda e, b=b, hp=hp, po=po, pp=pp: e.matmul(po[hp, 0:1], lhsT=vnew[0:1, b, hp], rhs=pp[0:1, 2:3], start=False, stop=True),
                                 reads=["vnew", ("pT", pi)], writes=[pok], signal=False)
                            P.op("pe", lambda e, hp=hp, po=po, pp=pp: e.matmul(po[hp, 2:3], lhsT=onesc[:, hp], rhs=pp[:, 0:1], start=True, stop=False),
                                 reads=["ones", ("pT", pi)], writes=[pok], signal=False)
                            P.op("pe", lambda e, hp=hp, po=po, pp=pp: e.matmul(po[hp, 2:3], lhsT=onesc[0:1, hp], rhs=pp[0:1, 2:3], start=False, stop=True),
                                 reads=["ones", ("pT", pi)], writes=[pok])
                            P.op("dve", lambda e, hp=hp, po=po, b=b: e.tensor_tensor(out=accO[hp, b:b + 1], in0=po[hp, 0:1], in1=accO[hp, b:b + 1], op=ALU.add),
                                 reads=[pok] + ALLACC, writes=ALLACC)
                            P.op("dve", lambda e, hp=hp, po=po, b=b: e.tensor_tensor(out=accL[hp, b:b + 1], in0=po[hp, 2:3], in1=accL[hp, b:b + 1], op=ALU.add),
                                 reads=[pok] + ALLACC, writes=ALLACC)
            P.op("dve", lambda e: e.reciprocal(out=accL, in_=accL), reads=ALLACC, writes=ALLACC)
            P.op("dve", lambda e, pr=pr: e.tensor_tensor(out=attnT[:, pr, :], in0=accO, in1=accL, op=ALU.mult), reads=ALLACC, writes=[("attnT", pr)])
        if DBG_DUMP and li == 0 and ci == 0 and not sample:
            P.op("sp", lambda e: e.dma_start(out=dbg_attn, in_=attnT[:]), reads=[("attnT", 0), ("attnT", 1), ("attnT", 2), ("attnT", 3)], writes=["OUT"], dma=True)
        wot = alloc([128, 4, D], BF16)
        P.op("sp", lambda e: e.dma_start(out=wot[:], in_=wo_b[li].rearrange("(k p) d -> p k d", p=128)), writes=["wot"], dma=True)
        for dc in range(8):
            for ti, (c0, n) in enumerate(tiles_of(T)):
                pt, pk = next_ps()
                for kc in range(4):
                    P.op("pe", lambda e, kc=kc, dc=dc, c0=c0, n=n, pt=pt: e.matmul(pt[:, 0:n], lhsT=wot[:, kc, dc * 128:(dc + 1) * 128], rhs=attnT[:, kc, c0:c0 + n], start=(kc == 0), stop=(kc == 3)),
                         reads=["wot", ("attnT", kc)], writes=[pk], signal=(kc == 3))
                P.op("dve", lambda e, dc=dc, c0=c0, n=n, pt=pt: e.tensor_tensor(out=x[:, dc, c0:c0 + n], in0=pt[:, 0:n], in1=x[:, dc, c0:c0 + n], op=ALU.add),
                     reads=[pk, ("x", dc, ti)], writes=[("x", dc, ti)])


    def run_chunk(ci, sample):
        T = NS if sample else TC
        P.barrier(keep=CONST + ["hprev0", "hprev1"])
        if sample:
            P.op("sp", lambda e: e.dma_start(out=x[:, :, 0:NS], in_=xsT.rearrange("(k p) t -> p k t", p=128)), writes=["xload"], dma=True)
        else:
            for k in range(8):
                P.op("sp", lambda e, k=k: e.dma_start(out=x[:, k, :], in_=xT[k * 128:(k + 1) * 128, ci * TC:(ci + 1) * TC]), writes=["xload"], dma=True)
        P.barrier(keep=CONST + ["hprev0", "hprev1"])
        for l in range(DBG_LAYERS):
            if l % 2 == 0:
                pool_layer(T, l, ci, sample)
            else:
                attn_layer(T, l, ci, sample)
            if not (DBG_SKIP_LAST_MLP and l == DBG_LAYERS - 1):
                mlp(T, l)
        P.barrier(keep=CONST + ["hprev0", "hprev1"])
        if sample:
            P.op("sp", lambda e: e.dma_start(out=ysT.rearrange("(k p) t -> p k t", p=128), in_=x[:, :, 0:NS]), writes=["OUT"], dma=True)
        else:
            for k in range(8):
                P.op("sp", lambda e, k=k: e.dma_start(out=yT[k * 128:(k + 1) * 128, ci * TC:(ci + 1) * TC], in_=x[:, k, :]), writes=["OUT"], dma=True)

    for ci in range(DBG_CHUNKS):
        run_chunk(ci, False)
    if DBG_SAMPLE:
        run_chunk(0, True)
    P.barrier(final=True)

    with nc.Block() as block:
        @block.tensor
        def _(e):
            for f in P.q["pe"]:
                f(e)

        @block.scalar
        def _(e):
            for f in P.q["act"]:
                f(e)

        @block.vector
        def _(e):
            for f in P.q["dve"]:
                f(e)

        @block.gpsimd
        def _(e):
            for f in P.q["pool"]:
                f(e)

        @block.sync
        def _(e):
            for f in P.q["sp"]:
                f(e)
    return nc


def _rope_tables():
    half = 32
    inv = (10000.0 ** (-np.arange(half, dtype=np.float32) * 2.0 / 64)).astype(np.float32)
    pos = np.concatenate([np.arange(SEQ), np.full(NS, SEQ)]).astype(np.float32)
    ang = pos[:, None] * inv[None, :]
    return np.cos(ang).astype(np.float32), np.sin(ang).astype(np.float32)


def kernel(x_prompt, x_sample, state_pool, cache_kv_w128, cache_kv_w512, cache_kv_w2048,
           norm_mix, norm_mlp, pool_w, pool_scale, attn_w_qkv, attn_q_norm, attn_k_norm,
           attn_w_o, mlp_w_up, mlp_w_down):
    f = lambda a: np.ascontiguousarray(np.asarray(a, dtype=np.float32))
    x_prompt, x_sample, state_pool = f(x_prompt), f(x_sample), f(state_pool)
    caches = [f(cache_kv_w128), f(cache_kv_w512), f(cache_kv_w2048)]
    cosT, sinT = _rope_tables()
    ii = np.arange(128)[:, None]
    jj = np.arange(128)[None, :]
    masks = np.concatenate([(jj <= ii), (jj >= ii)], axis=1).astype(np.float32).astype(ml_dtypes.bfloat16)
    invc = np.zeros((128, 8, 16), np.float32)
    for k in range(8):
        w = WIN[k // 2]
        invc[:, k, :] = 1.0 / np.minimum(np.arange(16) + 1, w).astype(np.float32)
    lay = lambda g: f(np.asarray(g, np.float32).reshape(g.shape[0], 8, 128).transpose(2, 0, 1))
    qkn = np.zeros((2, 3, 2, 128, 512), np.float32)
    qn_, kn_ = f(attn_q_norm), f(attn_k_norm)
    for li in range(2):
        for g in range(3):
            qkn[li, g, 0] = np.tile(qn_[li, g], (128, 8))
            qkn[li, g, 1] = np.tile(kn_[li, g], (128, 8))
    shared = {
        "gmix": lay(f(norm_mix)), "gmlp": lay(f(norm_mlp)), "pscale": lay(f(pool_scale)),
        "pool_w": f(pool_w), "wqkv": f(attn_w_qkv), "wo": f(attn_w_o), "wup": f(mlp_w_up), "wdn": f(mlp_w_down),
        "qkn": qkn, "sintab": sinT, "costab2": np.ascontiguousarray(np.concatenate([cosT, cosT], axis=1)), "nsintab": np.ascontiguousarray(-sinT), "masks": masks, "invc": invc,
        "identb": np.eye(128, dtype=np.float32).astype(ml_dtypes.bfloat16), "identf": np.eye(128, dtype=np.float32),
    }
    in_maps = []
    for c in range(NCORES):
        m = dict(shared)
        m["xT"] = f(x_prompt[c].T) if c < 2 else np.zeros((D, SEQ), np.float32)
        sl = slice(NS * c, NS * (c + 1))
        m["xsT"] = f(x_sample[sl, 0, :].T)
        m["stT"] = f(state_pool[:, sl].transpose(0, 3, 1, 2))
        m["st"] = f(state_pool[:, sl])
        m["c128"] = f(caches[0][:, sl].reshape(2, NS, 128, 2, 512))
        m["c512"] = f(caches[1][:, sl].reshape(2, NS, 512, 2, 512))
        m["c2048"] = f(caches[2][:, sl].reshape(2, NS, 2048, 2, 512))
        in_maps.append(m)
    nc = build_nc()
    res = run_bass_kernel_spmd(nc, in_maps, core_ids=list(range(NCORES))).results
    if DBG_DUMP:
        global LAST_RES
        LAST_RES = res
    y_prompt = np.stack([res[b]["yT"].T for b in range(2)]).astype(np.float32)
    y_sample = np.concatenate([res[c]["ysT"].T for c in range(NCORES)], axis=0)[:, None, :].astype(np.float32)
    pool_p = np.stack([res[b]["poolp"] for b in range(2)], axis=1).astype(np.float32)
    pool_s = np.concatenate([res[c]["pools"] for c in range(NCORES)], axis=1).astype(np.float32)
    outs = [y_prompt, y_sample, pool_p, pool_s]
    for g, (nm, nb) in enumerate((("128", 128), ("512", 512), ("2048", 2048))):
        kp = np.stack([res[b]["kp" + nm] for b in range(2)], axis=1).reshape(2, 2, nb, 2, 8, 64)
        ks = np.concatenate([res[c]["ks" + nm] for c in range(NCORES)], axis=1).reshape(2, NCORES * NS, nb, 2, 8, 64)
        outs += [kp.astype(np.float32), ks.astype(np.float32)]
    return tuple(np.ascontiguousarray(o) for o in outs)
```

```python
import numpy as np
import ml_dtypes
import concourse.bass as bass
import concourse.mybir as mybir
from concourse.bass_utils import run_bass_kernel_spmd

F32 = mybir.dt.float32
BF16 = mybir.dt.bfloat16
AF = mybir.ActivationFunctionType
ALU = mybir.AluOpType
AX = mybir.AxisListType

NCORES = 8
D = 1024
FF = 4096
SEQ = 8192
TC = 2048
NS = 4
EPS = 1e-6
DIL = [1, 4, 16]
NBUF = [128, 512, 2048]
WIN = [2, 4, 8, 16]
ENGS = ["pe", "act", "dve", "pool", "sp"]
import os
DBG_LAYERS = int(os.environ.get("KDBG_LAYERS", "4"))
DBG_SKIP_LAST_MLP = int(os.environ.get("KDBG_SKIPMLP", "0"))
DBG_CHUNKS = int(os.environ.get("KDBG_CHUNKS", "4"))
DBG_SAMPLE = int(os.environ.get("KDBG_SAMPLE", "1"))
DBG_DUMP = int(os.environ.get("KDBG_DUMP", "0"))
EPOCH = 8000


def ap_bcast_last(a, n):
    return bass.AP(a.tensor, a.offset, [list(x) for x in a.ap] + [[0, n]])


def ap_bcast_mid(a, n):
    l = [list(x) for x in a.ap]
    return bass.AP(a.tensor, a.offset, [l[0], [0, n]] + l[1:])


class Prog:
    def __init__(self, nc):
        self.nc = nc
        self.q = {e: [] for e in ENGS}
        self.cnt = {e: 0 for e in ENGS}
        self.waited = {e: {} for e in ENGS}
        self.lastw = {}
        self.readers = {}
        self.dmacnt = {}
        self.sems = {}
        self.pend_r = {e: set() for e in ENGS}
        self.pend_w = {e: set() for e in ENGS}
        self.last_tok = {e: None for e in ENGS}

    def sem(self, key):
        if key not in self.sems:
            self.sems[key] = self.nc.alloc_semaphore(name="s%d" % len(self.sems))
        return self.sems[key]

    def _need(self, eng, tok, is_dma):
        if tok is None:
            return None
        key, val, teng, tdma = tok
        if (not is_dma) and (not tdma) and teng == eng and eng == "pe":
            return None
        if self.waited[eng].get(key, 0) >= val:
            return None
        self.waited[eng][key] = val
        return (key, val)

    def op(self, eng, fn, reads=(), writes=(), signal=True, dma=False, nowaw=False):
        waits = []
        for b in reads:
            w = self._need(eng, self.lastw.get(b), dma)
            if w:
                waits.append(w)
        for b in writes:
            if nowaw:
                continue
            w = self._need(eng, self.lastw.get(b), dma)
            if w:
                waits.append(w)
            for t in self.readers.get(b, ()):
                w = self._need(eng, t, dma)
                if w:
                    waits.append(w)
        tok = None
        if dma:
            assert len(writes) == 1
            wb = writes[0]
            self.dmacnt[wb] = self.dmacnt.get(wb, 0) + 1
            tok = (("dma", wb), 16 * self.dmacnt[wb], eng, True)
        elif signal:
            self.cnt[eng] += 1
            ep = (self.cnt[eng] - 1) // EPOCH
            tok = (("eng", eng, ep), self.cnt[eng] - ep * EPOCH, eng, False)
            self.last_tok[eng] = tok
        wl = [(self.sem(k), v) for k, v in waits]
        ts = (self.sem(tok[0]), 16 if dma else 1) if tok else None

        def emit(e, wl=wl, fn=fn, ts=ts):
            for s, v in wl:
                e.wait_ge(s, v)
            ins = fn(e)
            if ts is not None:
                ins.then_inc(ts[0], ts[1])
        self.q[eng].append(emit)
        if tok is None:
            self.pend_r[eng].update(reads)
            self.pend_w[eng].update(writes)
        else:
            rs = set(reads)
            ws = set(writes)
            if not dma:
                rs |= self.pend_r[eng]
                ws |= self.pend_w[eng]
                self.pend_r[eng] = set()
                self.pend_w[eng] = set()
            for b in ws:
                self.lastw[b] = tok
                self.readers[b] = []
            for b in rs:
                if b not in ws:
                    self.readers.setdefault(b, []).append(tok)

    def barrier(self, keep=(), final=False):
        toks = {}
        for e in ENGS:
            t = self.last_tok[e]
            if t is not None:
                toks[t[0]] = max(toks.get(t[0], (0,))[0], t[1]), t
        for b, t in self.lastw.items():
            if t[3] and (final or b != "OUTC"):
                toks[t[0]] = max(toks.get(t[0], (0,))[0], t[1]), t
        for b, l in self.readers.items():
            for t in l:
                if t[3]:
                    toks[t[0]] = max(toks.get(t[0], (0,))[0], t[1]), t
        for e in ENGS:
            wl = []
            for key, (val, t) in toks.items():
                if (not t[3]) and t[2] == e:
                    continue
                if self.waited[e].get(key, 0) >= val:
                    continue
                self.waited[e][key] = val
                wl.append((self.sem(key), val))
            if wl:
                def emit(en, wl=wl):
                    for s, v in wl:
                        en.wait_ge(s, v)
                self.q[e].append(emit)
        keepw = {b: self.lastw[b] for b in list(keep) + ["OUTC"] if b in self.lastw}
        self.lastw = dict(keepw)
        self.readers = {}

    def finish(self, outbufs):
        wl = []
        for b in outbufs:
            t = self.lastw.get(b)
            if t is not None:
                wl.append((self.sem(t[0]), t[1]))

        def emit(e, wl=wl):
            for s, v in wl:
                e.wait_ge(s, v)
        self.q["sp"].append(emit)


def build_nc():
    nc = bass.Bass("TRN2", target_bir_lowering=False)
    P = Prog(nc)

    def din(name, shape, dt=F32):
        return nc.dram_tensor(name, list(shape), dt, kind="ExternalInput").ap()

    def dout(name, shape):
        return nc.dram_tensor(name, list(shape), F32, kind="ExternalOutput").ap()

    def dscr(name, shape, dt):
        if DBG_DUMP and name[2] == "0":
            return nc.dram_tensor(name, list(shape), dt, kind="ExternalOutput").ap()
        return nc.dram_tensor(name, list(shape), dt).ap()
    dbg_attn = nc.dram_tensor("dbg_attn", [128, 4, TC], BF16, kind="ExternalOutput").ap() if DBG_DUMP else None

    xT = din("xT", [D, SEQ])
    xsT = din("xsT", [D, NS])
    stT = din("stT", [2, D, NS, 15])
    st = din("st", [2, NS, 15, D])
    cch = [din("c128", [2, NS, 128, 2, 512]), din("c512", [2, NS, 512, 2, 512]), din("c2048", [2, NS, 2048, 2, 512])]
    gmix = din("gmix", [128, 4, 8])
    gmlp = din("gmlp", [128, 4, 8])
    pscale = din("pscale", [128, 2, 8])
    pool_w = din("pool_w", [2, 4, 256, 256])
    wqkv = din("wqkv", [2, D, 4608])
    wo = din("wo", [2, 512, D])
    wup = din("wup", [4, D, FF])
    wdn = din("wdn", [4, FF, D])
    qkn = din("qkn", [2, 3, 2, 128, 512])
    sintab = din("sintab", [SEQ + NS, 32])
    costab2 = din("costab2", [SEQ + NS, 64])
    nsintab = din("nsintab", [SEQ + NS, 32])
    masks_d = din("masks", [128, 256], BF16)
    invc_d = din("invc", [128, 8, 16])
    identb_d = din("identb", [128, 128], BF16)
    identf_d = din("identf", [128, 128])

    yT = dout("yT", [D, SEQ])
    ysT = dout("ysT", [D, NS])
    poolp = dout("poolp", [2, 15, D])
    pools = dout("pools", [2, NS, 15, D])
    kvp = [dout("kp128", [2, 128, 2, 512]), dout("kp512", [2, 512, 2, 512]), dout("kp2048", [2, 2048, 2, 512])]
    kvs = [dout("ks128", [2, NS, 128, 2, 512]), dout("ks512", [2, NS, 512, 2, 512]), dout("ks2048", [2, NS, 2048, 2, 512])]

    Ks = [[dscr("Ks%d%d" % (li, g), [SEQ + 16, 512], BF16) for g in range(3)] for li in range(2)]
    Vs = [[dscr("Vs%d%d" % (li, g), [SEQ + 16, 512], BF16) for g in range(3)] for li in range(2)]
    Qs = [[dscr("Qs%d%d" % (li, g), [SEQ + 16, 512], BF16) for g in range(3)] for li in range(2)]

    wup_b = dscr("wupb", [4, D, FF], BF16)
    wdn_b = dscr("wdnb", [4, FF, D], BF16)
    wqkv_b = dscr("wqkvb", [2, D, 4608], BF16)
    wo_b = dscr("wob", [2, 512, D], BF16)
    poolw_b = dscr("poolwb", [2, 4, 256, 256], BF16)

    def convert(src, dst, pat, kw, nsplit, key):
        sv = src.rearrange(pat, **kw)
        dv = dst.rearrange(pat, **kw)
        rows = sv.shape[0]
        step = rows // nsplit
        for i in range(nsplit):
            P.op("pool", lambda e, sv=sv, dv=dv, i=i, step=step: e.dma_start(out=dv[i * step:(i + 1) * step, :], in_=sv[i * step:(i + 1) * step, :]),
                 writes=[key], dma=True)

    for l in range(4):
        convert(wup[l], wup_b[l], "d (a f) -> (d a) f", dict(f=2048), 4, "cv_up")
        convert(wdn[l], wdn_b[l], "(r a) d -> r (a d)", dict(a=2), 4, "cv_dn")
    for li in range(2):
        convert(wqkv[li], wqkv_b[li], "d (a f) -> (d a) f", dict(f=1536), 4, "cv_qkv")
        convert(wo[li], wo_b[li], "r d -> r d", dict(), 1, "cv_o")
        convert(pool_w[li], poolw_b[li], "g c e -> (g c) e", dict(), 1, "cv_p")

    base = ((nc.sbuf_base + 63) // 64) * 64
    top = nc.sbuf_top
    state = {"off": base, "n": 0}

    def alloc(shape, dt):
        esz = 4 if dt == F32 else 2
        nb = esz
        for s in shape[1:]:
            nb *= s
        nb = ((nb + 63) // 64) * 64
        off = state["off"]
        assert off + nb <= top, ("SBUF overflow", off, nb, top)
        state["off"] = off + nb
        state["n"] += 1
        return nc.alloc_sbuf_tensor_at("t%d" % state["n"], list(shape), dt, offset=off)

    x = alloc([128, 8, TC], F32)
    h = alloc([128, 8, TC], BF16)
    ones_bf = alloc([128, 128], BF16)
    identb = alloc([128, 128], BF16)
    identf = alloc([128, 128], F32)
    masks = alloc([128, 256], BF16)
    invc = alloc([128, 8, 16], F32)
    g_mix = alloc([128, 4, 8], F32)
    g_mlp = alloc([128, 4, 8], F32)
    p_scale = alloc([128, 2, 8], F32)
    hprev = alloc([128, 2, 8, 15], F32)
    phase_base = state["off"]

    ps = [nc.alloc_psum_tensor("ps%d" % i, [128, 512], F32) for i in range(7)]
    psb = nc.alloc_psum_tensor("psb", [128, 1024], BF16)
    psrr = {"i": 0}

    def next_ps():
        i = psrr["i"] % 7
        psrr["i"] += 1
        return ps[i], ("ps", i)

    P.op("sp", lambda e: e.dma_start(out=identb[:], in_=identb_d), writes=["identb"], dma=True)
    P.op("sp", lambda e: e.dma_start(out=identf[:], in_=identf_d), writes=["identf"], dma=True)
    P.op("sp", lambda e: e.dma_start(out=masks[:], in_=masks_d), writes=["masks"], dma=True)
    P.op("sp", lambda e: e.dma_start(out=invc[:], in_=invc_d), writes=["invc"], dma=True)
    P.op("sp", lambda e: e.dma_start(out=g_mix[:], in_=gmix), writes=["g_mix"], dma=True)
    P.op("sp", lambda e: e.dma_start(out=g_mlp[:], in_=gmlp), writes=["g_mlp"], dma=True)
    P.op("sp", lambda e: e.dma_start(out=p_scale[:], in_=pscale), writes=["p_scale"], dma=True)
    P.op("dve", lambda e: e.memset(ones_bf[:], 1.0), writes=["ones"])
    for li_ in range(2):
        for g_ in range(3):
            for b_ in range(NS):
                P.op("sp", lambda e, g_=g_, li_=li_, b_=b_: e.dma_start(out=kvs[g_][li_, b_, 0:NBUF[g_] - 1, :, :], in_=cch[g_][li_, b_, 1:NBUF[g_], :, :]),
                     writes=["OUTC"], dma=True, nowaw=True)
    CONST = ["identb", "identf", "masks", "invc", "g_mix", "g_mlp", "p_scale", "ones"]

    def tiles_of(T):
        if T >= 512:
            return [(i * 512, 512) for i in range(T // 512)]
        return [(0, T)]

    def rmsnorm(T, gt, gkey, l, out_fn, sq, rs, extra_w):
        for ti, (c0, n) in enumerate(tiles_of(T)):
            for k in range(8):
                P.op("act", lambda e, k=k, c0=c0, n=n: e.activation(out=sq[:, k, 0:n], in_=x[:, k, c0:c0 + n], func=AF.Square),
                     reads=[("x", k, ti)], writes=[("sq", k)])
            pt, pk = next_ps()
            for k in range(8):
                P.op("pe", lambda e, k=k, n=n, pt=pt: e.matmul(pt[:, 0:n], lhsT=ones_bf[:], rhs=sq[:, k, 0:n], start=(k == 0), stop=(k == 7)),
                     reads=[("sq", k), "ones"], writes=[pk], signal=(k == 7))
            P.op("act", lambda e, n=n, pt=pt: e.activation(out=rs[:, 0:n], in_=pt[:, 0:n], func=AF.Sqrt, bias=EPS, scale=1.0 / D),
                 reads=[pk], writes=["rs"])
            P.op("dve", lambda e, n=n: e.reciprocal(out=rs[:, 0:n], in_=rs[:, 0:n]), reads=["rs"], writes=["rs"])
            for k in range(8):
                o = out_fn(k, c0, n)
                P.op("dve", lambda e, k=k, c0=c0, n=n, o=o: e.scalar_tensor_tensor(out=o, in0=x[:, k, c0:c0 + n], scalar=gt[:, l, k:k + 1], in1=rs[:, 0:n], op0=ALU.mult, op1=ALU.mult),
                     reads=[("x", k, ti), "rs", gkey], writes=[extra_w(k, ti)])

    def mlp(T, l):
        state["off"] = phase_base
        P.barrier(keep=CONST)
        sq = alloc([128, 8, 512], BF16)
        rs = alloc([128, 512], F32)
        rmsnorm(T, g_mlp, "g_mlp", l, lambda k, c0, n: h[:, k, c0:c0 + n], sq, rs, lambda k, ti: ("h", k, ti))
        hid = [alloc([128, 4, T], BF16) for _ in range(2)]
        NW = 3
        wu = [alloc([128, 8, 512], BF16) for _ in range(NW)]
        wd = [alloc([128, 4, D], BF16) for _ in range(NW)]
        NRL = 4
        rl = [alloc([128, 512], F32) for _ in range(NRL)]
        tl = tiles_of(T)
        rlc = {"i": 0}

        def load_w(fb):
            s = fb % NW
            P.op("sp", lambda e: e.dma_start(out=wu[s][:], in_=wup_b[l].rearrange("(k p) f -> p k f", p=128)[:, :, fb * 512:(fb + 1) * 512]),
                 writes=[("wu", s)], dma=True)
            P.op("sp", lambda e: e.dma_start(out=wd[s][:], in_=wdn_b[l, fb * 512:(fb + 1) * 512, :].rearrange("(k p) d -> p k d", p=128)),
                 writes=[("wd", s)], dma=True)

        def up(fb):
            s = fb % NW
            hs = fb % 2
            for fc in range(4):
                for ti, (c0, n) in enumerate(tl):
                    pt, pk = next_ps()
                    for k in range(8):
                        P.op("pe", lambda e, k=k, c0=c0, n=n, pt=pt, fc=fc: e.matmul(pt[:, 0:n], lhsT=wu[s][:, k, fc * 128:(fc + 1) * 128], rhs=h[:, k, c0:c0 + n], start=(k == 0), stop=(k == 7)),
                             reads=[("wu", s), ("h", k, ti)], writes=[pk], signal=(k == 7))
                    r = rlc["i"] % NRL
                    rlc["i"] += 1
                    P.op("act", lambda e, n=n, pt=pt, r=r: e.activation(out=rl[r][:, 0:n], in_=pt[:, 0:n], func=AF.Relu),
                         reads=[pk], writes=[("rl", r)])
                    P.op("pool", lambda e, n=n, c0=c0, r=r, fc=fc: e.tensor_tensor(out=hid[hs][:, fc, c0:c0 + n], in0=rl[r][:, 0:n], in1=rl[r][:, 0:n], op=ALU.mult),
                         reads=[("rl", r)], writes=[("hid", hs, fc, ti)])

        def down(fb):
            s = fb % NW
            hs = fb % 2
            for dc in range(8):
                for ti, (c0, n) in enumerate(tl):
                    pt, pk = next_ps()
                    for fc in range(4):
                        P.op("pe", lambda e, fc=fc, c0=c0, n=n, pt=pt, dc=dc: e.matmul(pt[:, 0:n], lhsT=wd[s][:, fc, dc * 128:(dc + 1) * 128], rhs=hid[hs][:, fc, c0:c0 + n], start=(fc == 0), stop=(fc == 3)),
                             reads=[("wd", s), ("hid", hs, fc, ti)], writes=[pk], signal=(fc == 3))
                    P.op("dve", lambda e, c0=c0, n=n, pt=pt, dc=dc: e.tensor_tensor(out=x[:, dc, c0:c0 + n], in0=pt[:, 0:n], in1=x[:, dc, c0:c0 + n], op=ALU.add),
                         reads=[pk, ("x", dc, ti)], writes=[("x", dc, ti)])

        load_w(0)
        load_w(1)
        up(0)
        for fb in range(8):
            if fb + 2 < 8:
                load_w(fb + 2)
            if fb + 1 < 8:
                up(fb + 1)
            down(fb)

    def pool_layer(T, l, ci, sample):
        li = l // 2
        state["off"] = phase_base
        P.barrier(keep=CONST + ["hprev0", "hprev1"])
        sq = alloc([128, 8, 512], BF16)
        rs = alloc([128, 512], F32)
        wp = alloc([128, 4, 2, 256], BF16)
        for gi in range(4):
            P.op("sp", lambda e, gi=gi: e.dma_start(out=wp[:, gi, :, :], in_=poolw_b[li, gi].rearrange("(c p) e -> p c e", p=128)), writes=["wp"], dma=True)
        if not sample:
            E = alloc([128, 8, 16 + 512], F32)
            W1 = alloc([128, 2, 16 + 512], F32)
            W2 = alloc([128, 2, 16 + 512], F32)
            tmp16 = alloc([128, 2, 16], F32)
            tl = tiles_of(T)
            for ti, (c0, n) in enumerate(tl):
                if ti == 0:
                    if ci == 0:
                        P.op("dve", lambda e: e.memset(E[:, :, 0:16], 0.0), writes=["Eh"])
                    else:
                        P.op("dve", lambda e: e.tensor_copy(out=E[:, :, 1:16], in_=hprev[:, li, :, :]), reads=["hprev%d" % li], writes=["Eh"])
                else:
                    P.op("dve", lambda e: e.tensor_copy(out=E[:, :, 1:16], in_=E[:, :, 513:528]), reads=["E"], writes=["Eh"])
                for k in range(8):
                    P.op("act", lambda e, k=k, c0=c0, n=n: e.activation(out=sq[:, k, 0:n], in_=x[:, k, c0:c0 + n], func=AF.Square),
                         reads=[("x", k, ti)], writes=[("sq", k)])
                pt, pk = next_ps()
                for k in range(8):
                    P.op("pe", lambda e, k=k, n=n, pt=pt: e.matmul(pt[:, 0:n], lhsT=ones_bf[:], rhs=sq[:, k, 0:n], start=(k == 0), stop=(k == 7)),
                         reads=[("sq", k), "ones"], writes=[pk], signal=(k == 7))
                P.op("act", lambda e, n=n, pt=pt: e.activation(out=rs[:, 0:n], in_=pt[:, 0:n], func=AF.Sqrt, bias=EPS, scale=1.0 / D),
                     reads=[pk], writes=["rs"])
                P.op("dve", lambda e, n=n: e.reciprocal(out=rs[:, 0:n], in_=rs[:, 0:n]), reads=["rs"], writes=["rs"])
                for k in range(8):
                    P.op("dve", lambda e, k=k, c0=c0, n=n: e.scalar_tensor_tensor(out=E[:, k, 16:16 + n], in0=x[:, k, c0:c0 + n], scalar=g_mix[:, l, k:k + 1], in1=rs[:, 0:n], op0=ALU.mult, op1=ALU.mult),
                         reads=[("x", k, ti), "rs", "g_mix", "Eh"], writes=["E"])
                L = 16 + n
                for gi in range(4):
                    w = WIN[gi]
                    k0 = 2 * gi
                    P.op("pool", lambda e, k0=k0, L=L: e.tensor_tensor(out=W1[:, :, 1:L], in0=E[:, k0:k0 + 2, 1:L], in1=E[:, k0:k0 + 2, 0:L - 1], op=ALU.add),
                         reads=["E", "Eh"], writes=["W1"])
                    cur = W1
                    ck = "W1"
                    if w >= 4:
                        P.op("pool", lambda e, L=L: e.tensor_tensor(out=W2[:, :, 3:L], in0=W1[:, :, 3:L], in1=W1[:, :, 1:L - 2], op=ALU.add),
                             reads=["W1"], writes=["W2"])
                        cur, ck = W2, "W2"
                    if w >= 8:
                        P.op("pool", lambda e, L=L: e.tensor_tensor(out=W1[:, :, 7:L], in0=W2[:, :, 7:L], in1=W2[:, :, 3:L - 4], op=ALU.add),
                             reads=["W2"], writes=["W1"])
                        cur, ck = W1, "W1"
                    if w >= 16:
                        P.op("pool", lambda e, L=L: e.tensor_tensor(out=W2[:, :, 15:L], in0=W1[:, :, 15:L], in1=W1[:, :, 7:L - 8], op=ALU.add),
                             reads=["W1"], writes=["W2"])
                        cur, ck = W2, "W2"
                    P.op("dve", lambda e, cur=cur, k0=k0, c0=c0, n=n, w=w: e.scalar_tensor_tensor(out=h[:, k0:k0 + 2, c0:c0 + n], in0=cur[:, :, 16:16 + n], scalar=1.0 / w, in1=E[:, k0:k0 + 2, 16:16 + n], op0=ALU.mult, op1=ALU.subtract),
                         reads=[ck, "E"], writes=[("h", k0, ti), ("h", k0 + 1, ti)])
                    if ci == 0 and ti == 0:
                        P.op("dve", lambda e, cur=cur, k0=k0: e.tensor_tensor(out=tmp16[:], in0=cur[:, :, 16:32], in1=invc[:, k0:k0 + 2, :], op=ALU.mult),
                             reads=[ck, "invc"], writes=["tmp16"])
                        P.op("dve", lambda e, k0=k0: e.tensor_tensor(out=h[:, k0:k0 + 2, 0:16], in0=tmp16[:], in1=E[:, k0:k0 + 2, 16:32], op=ALU.subtract),
                             reads=["tmp16", "E"], writes=[("h", k0, ti), ("h", k0 + 1, ti)])
                if ti == len(tl) - 1:
                    P.op("dve", lambda e, n=n: e.tensor_copy(out=hprev[:, li, :, :], in_=E[:, :, 16 + n - 15:16 + n]), reads=["E"], writes=["hprev%d" % li])
                    if ci == 3:
                        rows = alloc([15, D], F32)
                        for hf in range(2):
                            pt, pk = next_ps()
                            for k in range(4):
                                P.op("pe", lambda e, k=k, n=n, pt=pt, hf=hf: e.transpose(pt[0:15, k * 128:(k + 1) * 128], E[:, hf * 4 + k, 16 + n - 15:16 + n], identf[:]),
                                     reads=["E", "identf"], writes=[pk], signal=(k == 3))
                            P.op("dve", lambda e, pt=pt, hf=hf: e.tensor_copy(out=rows[:, hf * 512:(hf + 1) * 512], in_=pt[0:15, :]), reads=[pk], writes=["prow"])
                        P.op("sp", lambda e: e.dma_start(out=poolp[li], in_=rows[:]), reads=["prow"], writes=["OUT"], dma=True)
                pool_matmul(wp, T, li, [(ti, c0, n)])
        else:
            Es = alloc([128, 8, NS, 16], F32)
            ssum = alloc([128, 2, NS], F32)
            rows = alloc([NS, D], F32)
            for k in range(8):
                P.op("sp", lambda e, k=k: e.dma_start(out=Es[:, k, :, 0:15], in_=stT[li, k * 128:(k + 1) * 128, :, :]), writes=["Es"], dma=True)
            rmsnorm(T, g_mix, "g_mix", l, lambda k, c0, n: Es[:, k, :, 15], sq, rs, lambda k, ti: "Es")
            for gi in range(4):
                w = WIN[gi]
                k0 = 2 * gi
                P.op("dve", lambda e, k0=k0, w=w: e.tensor_reduce(out=ssum[:], in_=Es[:, k0:k0 + 2, :, 16 - w:16], axis=AX.X, op=ALU.add),
                     reads=["Es"], writes=["ssum"])
                P.op("dve", lambda e, k0=k0, w=w: e.scalar_tensor_tensor(out=h[:, k0:k0 + 2, 0:NS], in0=ssum[:], scalar=1.0 / w, in1=Es[:, k0:k0 + 2, :, 15], op0=ALU.mult, op1=ALU.subtract),
                     reads=["ssum", "Es"], writes=[("h", k0, 0), ("h", k0 + 1, 0)])
            P.op("sp", lambda e: e.dma_start(out=pools[li, :, 0:14, :], in_=st[li, :, 1:15, :]), writes=["OUT"], dma=True)
            for hf in range(2):
                pt, pk = next_ps()
                for k in range(4):
                    P.op("pe", lambda e, k=k, pt=pt, hf=hf: e.transpose(pt[0:NS, k * 128:(k + 1) * 128], Es[:, hf * 4 + k, :, 15], identf[:]),
                         reads=["Es", "identf"], writes=[pk], signal=(k == 3))
                P.op("dve", lambda e, pt=pt, hf=hf: e.tensor_copy(out=rows[:, hf * 512:(hf + 1) * 512], in_=pt[0:NS, :]), reads=[pk], writes=["prow"])
            P.op("sp", lambda e: e.dma_start(out=pools[li, :, 14, :], in_=rows[:]), reads=["prow"], writes=["OUT"], dma=True)
            pool_matmul(wp, T, li, [(0, 0, NS)])

    def pool_matmul(wp, T, li, tl):
        for ti, c0, n in tl:
            for gi in range(4):
                for ec in range(2):
                    pt, pk = next_ps()
                    for cc in range(2):
                        P.op("pe", lambda e, gi=gi, ec=ec, cc=cc, c0=c0, n=n, pt=pt: e.matmul(pt[:, 0:n], lhsT=wp[:, gi, cc, ec * 128:(ec + 1) * 128], rhs=h[:, 2 * gi + cc, c0:c0 + n], start=(cc == 0), stop=(cc == 1)),
                             reads=["wp", ("h", 2 * gi + cc, ti)], writes=[pk], signal=(cc == 1))
                    kk = 2 * gi + ec
                    P.op("dve", lambda e, kk=kk, c0=c0, n=n, pt=pt: e.scalar_tensor_tensor(out=x[:, kk, c0:c0 + n], in0=pt[:, 0:n], scalar=p_scale[:, li, kk:kk + 1], in1=x[:, kk, c0:c0 + n], op0=ALU.mult, op1=ALU.add),
                         reads=[pk, ("x", kk, ti), "p_scale"], writes=[("x", kk, ti)])

    def attn_layer(T, l, ci, sample):
        li = l // 2
        state["off"] = phase_base
        P.barrier(keep=CONST)
        sq = alloc([128, 8, 512], BF16)
        rs = alloc([128, 512], F32)
        rmsnorm(T, g_mix, "g_mix", l, lambda k, c0, n: h[:, k, c0:c0 + n], sq, rs, lambda k, ti: ("h", k, ti))
        row0 = SEQ if sample else ci * TC
        nblk = 1 if sample else T // 128
        npb = NS if sample else 128
        wq = [alloc([128, 8, 512], BF16) for _ in range(2)]
        gn = [alloc([128, 512], F32) for _ in range(2)]
        cs = alloc([128, 16, 32], F32)
        sn = alloc([128, 16, 32], F32)
        qf = [alloc([128, 512], F32) for _ in range(4)]
        qn = [alloc([128, 512], F32) for _ in range(4)]
        sqf2 = [alloc([128, 512], F32) for _ in range(4)]
        ssh2 = [alloc([128, 8], F32) for _ in range(4)]
        t1a2 = [alloc([128, 8, 32], F32) for _ in range(4)]
        t2a2 = [alloc([128, 8, 32], F32) for _ in range(4)]
        t1b2 = [alloc([128, 8, 32], F32) for _ in range(4)]
        t2b2 = [alloc([128, 8, 32], F32) for _ in range(4)]
        sqf = sqf2[0]
        qb16 = [alloc([128, 512], BF16) for _ in range(4)]
        ssh = alloc([128, 8], F32)
        t1 = alloc([128, 8, 32], F32)
        t2 = alloc([128, 8, 32], F32)
        cs2 = alloc([128, 16, 64], F32)
        nsn = alloc([128, 16, 32], F32)
        t2f = [alloc([128, 512], F32) for _ in range(4)]
        P.op("sp", lambda e: e.dma_start(out=cs2[0:npb, 0:nblk, :], in_=costab2[row0:row0 + npb * nblk, :].rearrange("(b p) c -> p b c", p=npb)), writes=["cs2"], dma=True)
        P.op("sp", lambda e: e.dma_start(out=nsn[0:npb, 0:nblk, :], in_=nsintab[row0:row0 + npb * nblk, :].rearrange("(b p) c -> p b c", p=npb)), writes=["nsn"], dma=True)
        P.op("sp", lambda e: e.dma_start(out=sn[0:npb, 0:nblk, :], in_=sintab[row0:row0 + npb * nblk, :].rearrange("(b p) c -> p b c", p=npb)), writes=["sn"], dma=True)
        NBF = 4
        ptmap = {}

        def S1(tb, s, part, g):
            pt, pk = next_ps()
            for k in range(8):
                P.op("pe", lambda e, k=k: e.matmul(pt[0:npb, :], lhsT=h[:, k, tb * npb:(tb + 1) * npb], rhs=wq[s][:, k, :], start=(k == 0), stop=(k == 7)),
                     reads=[("wq", s), ("h", k, tb // 4)], writes=[pk], signal=(k == 7))
            r = tb % NBF
            of = qf[r]
            if part == 2:
                P.op("act", lambda e: e.activation(out=of[0:npb, :], in_=pt[0:npb, :], func=AF.Copy), reads=[pk], writes=[("qf", r)])
                return
            sqf_, ssh_, qq = sqf2[r], ssh2[r], qn[r]
            ptmap[tb] = (pt, pk)
            P.op("act", lambda e: e.activation(out=sqf_[0:npb, :], in_=pt[0:npb, :], func=AF.Square), reads=[pk], writes=[("sqf", r)])
            P.op("dve", lambda e: e.tensor_reduce(out=ssh_[0:npb, :], in_=sqf_[0:npb, :].rearrange("p (h d) -> p h d", d=64), axis=AX.X, op=ALU.add), reads=[("sqf", r)], writes=[("ssh", r)])

        def S1b(tb, s, part, g):
            if part == 2:
                return
            r = tb % NBF
            ssh_, qq = ssh2[r], qn[r]
            pt, pk = ptmap[tb]
            P.op("act", lambda e: e.activation(out=ssh_[0:npb, :], in_=ssh_[0:npb, :], func=AF.Sqrt, bias=EPS, scale=1.0 / 64), reads=[("ssh", r)], writes=[("ssh", r)])
            P.op("dve", lambda e: e.reciprocal(out=ssh_[0:npb, :], in_=ssh_[0:npb, :]), reads=[("ssh", r)], writes=[("ssh", r)])
            P.op("dve", lambda e: e.tensor_tensor(out=qq[0:npb, :].rearrange("p (h d) -> p h d", d=64), in0=pt[0:npb, :].rearrange("p (h d) -> p h d", d=64), in1=ap_bcast_last(ssh_[0:npb, :], 64), op=ALU.mult),
                 reads=[pk, ("ssh", r)], writes=[("qn", r)])
            P.op("pool", lambda e: e.tensor_tensor(out=qq[0:npb, :], in0=qq[0:npb, :], in1=gn[s][0:npb, :], op=ALU.mult),
                 reads=[("qn", r), ("gn", s)], writes=[("qn", r)])

        def S2(tb, s, part, g):
            if part == 2:
                return
            r = tb % NBF
            qq = qn[r]
            t1f, t2f_ = sqf2[r], t2f[r]
            q3 = qq[0:npb, :].rearrange("p (h d) -> p h d", d=64)
            t13 = t1f[0:npb, :].rearrange("p (h d) -> p h d", d=64)
            t23 = t2f_[0:npb, :].rearrange("p (h d) -> p h d", d=64)
            cb = ap_bcast_mid(cs2[0:npb, tb, :], 8)
            sb = ap_bcast_mid(sn[0:npb, tb, :], 8)
            nsb = ap_bcast_mid(nsn[0:npb, tb, :], 8)
            P.op("dve", lambda e: e.tensor_tensor(out=t13, in0=q3, in1=cb, op=ALU.mult), reads=[("qn", r), "cs2"], writes=[("sqf", r)])
            P.op("pool", lambda e: e.tensor_tensor(out=t23[:, :, 0:32], in0=q3[:, :, 32:64], in1=nsb, op=ALU.mult), reads=[("qn", r), "nsn"], writes=[("t2f", r)])
            P.op("pool", lambda e: e.tensor_tensor(out=t23[:, :, 32:64], in0=q3[:, :, 0:32], in1=sb, op=ALU.mult), reads=[("qn", r), "sn"], writes=[("t2f", r)])

        def S3(tb, s, part, g):
            r = tb % NBF
            of = qf[r]
            if part < 2:
                t1f, t2f_ = sqf2[r], t2f[r]
                P.op("dve", lambda e: e.tensor_tensor(out=of[0:npb, :], in0=t1f[0:npb, :], in1=t2f_[0:npb, :], op=ALU.add), reads=[("sqf", r), ("t2f", r)], writes=[("qf", r)])
            scr = [Qs, Ks, Vs][part][li][g]
            r0 = row0 + tb * npb
            ob = qb16[r]
            P.op("act", lambda e: e.activation(out=ob[0:npb, :], in_=of[0:npb, :], func=AF.Copy),
                 reads=[("qf", r)], writes=[("qb16", r)])
            P.op("sp", lambda e: e.dma_start(out=scr[r0:r0 + npb, :], in_=ob[0:npb, :]),
                 reads=[("qb16", r)], writes=[("scr", part, g)], dma=True)
            if part >= 1:
                nb = NBUF[g]
                if sample:
                    P.op("sp", lambda e: e.dma_start(out=kvs[g][li, :, nb - 1, part - 1, :], in_=of[0:NS, :]),
                         reads=[("qf", r)], writes=["OUT"], dma=True)
                else:
                    pos0 = ci * TC + tb * 128
                    if pos0 >= SEQ - nb:
                        rr = pos0 - (SEQ - nb)
                        P.op("sp", lambda e: e.dma_start(out=kvp[g][li, rr:rr + 128, part - 1, :], in_=of[:, :]),
                             reads=[("qf", r)], writes=["OUT"], dma=True)

        combo = 0
        for g in range(3):
            for part in range(3):
                s = combo % 2
                combo += 1
                col0 = part * 1536 + g * 512
                P.op("sp", lambda e, s=s, col0=col0: e.dma_start(out=wq[s][:], in_=wqkv_b[li].rearrange("(k p) f -> p k f", p=128)[:, :, col0:col0 + 512]),
                     writes=[("wq", s)], dma=True)
                if part < 2:
                    P.op("sp", lambda e, s=s, g=g, part=part: e.dma_start(out=gn[s][:], in_=qkn[li, g, part]), writes=[("gn", s)], dma=True)
                for t in range(nblk + 3):
                    if 0 <= t - 3 < nblk:
                        S3(t - 3, s, part, g)
                    if 0 <= t - 2 < nblk:
                        S2(t - 2, s, part, g)
                    if 0 <= t - 1 < nblk:
                        S1b(t - 1, s, part, g)
                    if t < nblk:
                        S1(t, s, part, g)
        state["off"] = phase_base
        P.barrier(keep=CONST)
        accOL = alloc([128, 2, T], F32)
        accO = accOL[:, 0, :]
        accL = accOL[:, 1, :]
        attnT = alloc([128, 4, T], BF16)
        onesc = ones_bf
        NB1 = 32
        qrows2 = [alloc([128, 16, 128], BF16) for _ in range(2)]
        krows2 = [alloc([128, NB1, 128], BF16) for _ in range(2)]
        vrows2 = [alloc([128, NB1, 128], BF16) for _ in range(2)]
        qT2 = [alloc([128, 16 * 128], BF16) for _ in range(2)]
        kT2 = [alloc([128, NB1 * 128], BF16) for _ in range(2)]
        qrows, krows, vrows, qT, kT = qrows2[0], krows2[0], vrows2[0], qT2[0], kT2[0]
        NPT = 6
        pT = [alloc([128, 256], BF16) for _ in range(NPT)]
        knew = alloc([NS, 128], BF16)
        vnew = alloc([1, NS, 128], BF16)
        scale = 64 ** -0.5
        clsn = {"i": 0, "it": 0}

        def acc_keys(nm, g, hh, qb):
            if g == 0:
                return [(nm, hh, qb // 4)]
            if g == 1:
                return [(nm, hh, qb)]
            return [(nm, hh, t) for t in range(4)]

        def class_setup(pc, g, cs_):
            Dg = DIL[g]
            nqb = (T // Dg) // 128
            first = ci * TC
            halo = (ci > 0)
            hoff = 1 if halo else 0
            nkc = nqb + hoff
            qr, kr, vr, qt, kt = qrows2[cs_], krows2[cs_], vrows2[cs_], qT2[cs_], kT2[cs_]
            if Dg == 1:
                kstart = first - (128 if halo else 0)
                P.op("sp", lambda e: e.dma_start(out=qr[:, 0:nqb, :], in_=Qs[li][g][first:first + 128 * nqb, pc].rearrange("(b p) c -> p b c", p=128)),
                     reads=[("scr", 0, g)], writes=[("qrows", cs_)], dma=True)
                P.op("sp", lambda e: e.dma_start(out=kr[:, 0:nkc, :], in_=Ks[li][g][kstart:kstart + 128 * nkc, pc].rearrange("(b p) c -> p b c", p=128)),
                     reads=[("scr", 1, g)], writes=[("krows", cs_)], dma=True)
                P.op("sp", lambda e: e.dma_start(out=vr[:, 0:nkc, :], in_=Vs[li][g][kstart:kstart + 128 * nkc, pc].rearrange("(b p) c -> p b c", p=128)),
                     reads=[("scr", 2, g)], writes=[("vrows", cs_)], dma=True)
            else:
                span = 128 * Dg
                for bb in range(nqb):
                    r0 = first + span * bb
                    P.op("sp", lambda e, bb=bb, r0=r0: e.dma_start(out=qr[:, 0:Dg * nqb, :].rearrange("p (r b) c -> p b r c", b=nqb)[:, bb, :, :], in_=Qs[li][g][r0:r0 + span, pc].rearrange("(p r) c -> p r c", r=Dg)),
                         reads=[("scr", 0, g)], writes=[("qrows", cs_)], dma=True)
                for bb in range(-hoff, nqb):
                    r0 = first + span * bb
                    for (scr_, t_, key_, part_) in ((Ks, kr, "krows", 1), (Vs, vr, "vrows", 2)):
                        P.op("sp", lambda e, bb=bb, r0=r0, scr_=scr_, t_=t_: e.dma_start(out=t_[:, 0:Dg * nkc, :].rearrange("p (r k) c -> p k r c", k=nkc)[:, bb + hoff, :, :], in_=scr_[li][g][r0:r0 + span, pc].rearrange("(p r) c -> p r c", r=Dg)),
                             reads=[("scr", part_, g)], writes=[(key_, cs_)], dma=True)
            for (src, skey, dst, dkey, nbk) in ((qr, ("qrows", cs_), qt, ("qT", cs_), Dg * nqb), (kr, ("krows", cs_), kt, ("kT", cs_), Dg * nkc)):
                for b0 in range(0, nbk, 8):
                    nn = min(8, nbk - b0)
                    for j in range(nn):
                        P.op("pe", lambda e, src=src, b0=b0, j=j: e.transpose(psb[:, j * 128:(j + 1) * 128], src[:, b0 + j, :], identb[:]),
                             reads=[skey, "identb"], writes=["psb"], signal=(j == nn - 1))
                    P.op("act", lambda e, dst=dst, b0=b0, nn=nn: e.activation(out=dst[:, b0 * 128:(b0 + nn) * 128], in_=psb[:, 0:nn * 128], func=AF.Copy),
                         reads=["psb"], writes=[dkey])
            items = []
            for r in range(Dg):
                for qb in range(nqb):
                    kbs = []
                    if halo or qb > 0:
                        kbs.append((r * nkc + qb - 1 + hoff, 0))
                    kbs.append((r * nkc + qb + hoff, 1))
                    for hh in range(2):
                        items.append(dict(g=g, rcl=r, cs=cs_, qb=qb, qi=r * nqb + qb, hh=hh, kbs=kbs, Dg=Dg))
            return items

        def stage_a(it):
            cs_, qb, hh, kbs = it["cs"], it["qi"], it["hh"], it["kbs"]
            hp = slice(hh * 64, (hh + 1) * 64)
            qt, kt = qT2[cs_], kT2[cs_]
            pt, pk = next_ps()
            for (kb, mi) in kbs:
                P.op("pe", lambda e, kb=kb, mi=mi: e.matmul(pt[:, mi * 128:(mi + 1) * 128], lhsT=kt[hp, kb * 128:(kb + 1) * 128], rhs=qt[hp, qb * 128:(qb + 1) * 128], start=True, stop=True),
                     reads=[("kT", cs_), ("qT", cs_)], writes=[pk], signal=(mi == 1))
            c_lo = 0 if len(kbs) == 2 else 128
            pi = clsn["it"] % NPT
            clsn["it"] += 1
            pp = pT[pi]
            it["pi"], it["pp"] = pi, pp
            P.op("act", lambda e: e.activation(out=pp[:, c_lo:256], in_=pt[:, c_lo:256], func=AF.Exp, scale=scale),
                 reads=[pk], writes=[("pT", pi)])
            meng = "dve" if (clsn["it"] % 4 == 0) else "pool"
            P.op(meng, lambda e: e.tensor_tensor(out=pp[:, c_lo:256], in0=pp[:, c_lo:256], in1=masks[:, c_lo:256], op=ALU.mult),
                 reads=[("pT", pi), "masks"], writes=[("pT", pi)])

        def stage_b(it):
            cs_, qb, hh, kbs, g, rcl, Dg = it["cs"], it["qb"], it["hh"], it["kbs"], it["g"], it["rcl"], it["Dg"]
            pi, pp = it["pi"], it["pp"]
            hp = slice(hh * 64, (hh + 1) * 64)
            vr = vrows2[cs_]
            po, pok = next_ps()
            nk = len(kbs)
            for i, (kb, mi) in enumerate(kbs):
                P.op("pe", lambda e, kb=kb, mi=mi, i=i: e.matmul(po[hp, 0:128], lhsT=vr[:, kb, hp], rhs=pp[:, mi * 128:(mi + 1) * 128], start=(i == 0), stop=(i == nk - 1)),
                     reads=[("vrows", cs_), ("pT", pi)], writes=[pok], signal=False)
            for i, (kb, mi) in enumerate(kbs):
                P.op("pe", lambda e, mi=mi, i=i: e.matmul(po[hp, 128:256], lhsT=onesc[:, hp], rhs=pp[:, mi * 128:(mi + 1) * 128], start=(i == 0), stop=(i == nk - 1)),
                     reads=["ones", ("pT", pi)], writes=[pok], signal=(i == nk - 1))
            aOL = accOL[hp, :, :].rearrange("p a (n d) -> p a d n", d=Dg)[:, :, rcl, qb * 128:(qb + 1) * 128]
            ks_ = acc_keys("acc", g, hh, qb)
            P.op("dve", lambda e: e.tensor_tensor(out=aOL, in0=po[hp, 0:256].rearrange("p (a n) -> p a n", a=2), in1=aOL, op=ALU.add),
                 reads=[pok] + ks_, writes=ks_)

        ALLACC = [("acc", hh, t) for hh in range(2) for t in range(4)]
        for pr in range(4):
            pc = slice(pr * 128, (pr + 1) * 128)
            P.op("dve", lambda e: e.memset(accOL[:], 0.0), writes=ALLACC)
            if not sample:
                classes = [0, 1, 2]
                cur = class_setup(pc, classes[0], clsn["i"] % 2)
                clsn["i"] += 1
                for cidx in range(len(classes)):
                    nxt = None
                    if cidx + 1 < len(classes):
                        nxt = class_setup(pc, classes[cidx + 1], clsn["i"] % 2)
                        clsn["i"] += 1
                    LOOK = 4
                    for i in range(min(LOOK, len(cur))):
                        stage_a(cur[i])
                    for i in range(len(cur)):
                        stage_b(cur[i])
                        if i + LOOK < len(cur):
                            stage_a(cur[i + LOOK])
                    cur = nxt
            for g in range(3):
                Dg = DIL[g]
                if not sample:
                    pass
                else:
                    P.op("sp", lambda e, pc=pc, g=g: e.dma_start(out=qrows[0:NS, 0, :], in_=Qs[li][g][SEQ:SEQ + NS, pc]), reads=[("scr", 0, g)], writes=["qrows"], dma=True)
                    P.op("sp", lambda e, pc=pc, g=g: e.dma_start(out=knew[:, :], in_=Ks[li][g][SEQ:SEQ + NS, pc]), reads=[("scr", 1, g)], writes=["knew"], dma=True)
                    P.op("sp", lambda e, pc=pc, g=g: e.dma_start(out=vnew[0:1, :, :], in_=Vs[li][g][SEQ:SEQ + NS, pc].rearrange("(o b) c -> o b c", o=1)), reads=[("scr", 2, g)], writes=["vnew"], dma=True)
                    P.op("pe", lambda e: e.transpose(psb[:, 0:NS], qrows[0:NS, 0, :], identb[0:NS, 0:NS]), reads=["qrows", "identb"], writes=["psb"], signal=False)
                    P.op("pe", lambda e: e.transpose(psb[:, 128:128 + NS], knew[:, :], identb[0:NS, 0:NS]), reads=["knew", "identb"], writes=["psb"])
                    P.op("act", lambda e: e.activation(out=qT[:, 0:NS], in_=psb[:, 0:NS], func=AF.Copy), reads=["psb"], writes=["qT"])
                    P.op("act", lambda e: e.activation(out=qT[:, 128:128 + NS], in_=psb[:, 128:128 + NS], func=AF.Copy), reads=["psb"], writes=["qT"])
                    for b in range(NS):
                        P.op("pool", lambda e, pc=pc, g=g, b=b, Dg=Dg: e.dma_start(out=krows[:, b, :], in_=cch[g][li, b].rearrange("(n d) t c -> d n t c", d=Dg)[0, :, 0, pc]), writes=["krows"], dma=True)
                        P.op("pool", lambda e, pc=pc, g=g, b=b, Dg=Dg: e.dma_start(out=vrows[:, b, :], in_=cch[g][li, b].rearrange("(n d) t c -> d n t c", d=Dg)[0, :, 1, pc]), writes=["vrows"], dma=True)
                    for b in range(NS):
                        P.op("pe", lambda e, b=b: e.transpose(psb[:, b * 128:(b + 1) * 128], krows[:, b, :], identb[:]), reads=["krows", "identb"], writes=["psb"], signal=(b == NS - 1))
                    P.op("act", lambda e: e.activation(out=kT[:, 0:NS * 128], in_=psb[:, 0:NS * 128], func=AF.Copy), reads=["psb"], writes=["kT"])
                    for b in range(NS):
                        for hh in range(2):
                            hp = slice(hh * 64, (hh + 1) * 64)
                            pt, pk = next_ps()
                            P.op("pe", lambda e, b=b, hp=hp, pt=pt: e.matmul(pt[:, 0:1], lhsT=kT[hp, b * 128:(b + 1) * 128], rhs=qT[hp, b:b + 1], start=True, stop=True),
                                 reads=["kT", "qT"], writes=[pk], signal=False)
                            P.op("pe", lambda e, b=b, hp=hp, pt=pt: e.matmul(pt[0:1, 2:3], lhsT=qT[hp, 128 + b:129 + b], rhs=qT[hp, b:b + 1], start=True, stop=True),
                                 reads=["qT"], writes=[pk])
                            pi = (b * 2 + hh) % 2
                            pp = pT[pi]
                            P.op("act", lambda e, pt=pt, pp=pp: e.activation(out=pp[:, 0:1], in_=pt[:, 0:1], func=AF.Exp, scale=scale), reads=[pk], writes=[("pT", pi)])
                            P.op("act", lambda e, pt=pt, pp=pp: e.activation(out=pp[0:1, 2:3], in_=pt[0:1, 2:3], func=AF.Exp, scale=scale), reads=[pk], writes=[("pT", pi)])
                            po, pok = next_ps()
                            P.op("pe", lambda e, b=b, hp=hp, po=po, pp=pp: e.matmul(po[hp, 0:1], lhsT=vrows[:, b, hp], rhs=pp[:, 0:1], start=True, stop=False),
                                 reads=["vrows", ("pT", pi)], writes=[pok], signal=False)
                            P.op("pe", lambda e, b=b, hp=hp, po=po, pp=pp: e.matmul(po[hp, 0:1], lhsT=vnew[0:1, b, hp], rhs=pp[0:1, 2:3], start=False, stop=True),
                                 reads=["vnew", ("pT", pi)], writes=[pok], signal=False)
                            P.op("pe", lambda e, hp=hp, po=po, pp=pp: e.matmul(po[hp, 2:3], lhsT=onesc[:, hp], rhs=pp[:, 0:1], start=True, stop=False),
                                 reads=["ones", ("pT", pi)], writes=[pok], signal=False)
                            P.op("pe", lambda e, hp=hp, po=po, pp=pp: e.matmul(po[hp, 2:3], lhsT=onesc[0:1, hp], rhs=pp[0:1, 2:3], start=False, stop=True),
                                 reads=["ones", ("pT", pi)], writes=[pok])
                            P.op("dve", lambda e, hp=hp, po=po, b=b: e.tensor_tensor(out=accO[hp, b:b + 1], in0=po[hp, 0:1], in1=accO[hp, b:b + 1], op=ALU.add),
                                 reads=[pok] + ALLACC, writes=ALLACC)
                            P.op("dve", lambda e, hp=hp, po=po, b=b: e.tensor_tensor(out=accL[hp, b:b + 1], in0=po[hp, 2:3], in1=accL[hp, b:b + 1], op=ALU.add),
                                 reads=[pok] + ALLACC, writes=ALLACC)
            P.op("dve", lambda e: e.reciprocal(out=accL, in_=accL), reads=ALLACC, writes=ALLACC)
            P.op("dve", lambda e, pr=pr: e.tensor_tensor(out=attnT[:, pr, :], in0=accO, in1=accL, op=ALU.mult), reads=ALLACC, writes=[("attnT", pr)])
        if DBG_DUMP and li == 0 and ci == 0 and not sample:
            P.op("sp", lambda e: e.dma_start(out=dbg_attn, in_=attnT[:]), reads=[("attnT", 0), ("attnT", 1), ("attnT", 2), ("attnT", 3)], writes=["OUT"], dma=True)
        wot = alloc([128, 4, D], BF16)
        P.op("sp", lambda e: e.dma_start(out=wot[:], in_=wo_b[li].rearrange("(k p) d -> p k d", p=128)), writes=["wot"], dma=True)
        for dc in range(8):
            for ti, (c0, n) in enumerate(tiles_of(T)):
                pt, pk = next_ps()
                for kc in range(4):
                    P.op("pe", lambda e, kc=kc, dc=dc, c0=c0, n=n, pt=pt: e.matmul(pt[:, 0:n], lhsT=wot[:, kc, dc * 128:(dc + 1) * 128], rhs=attnT[:, kc, c0:c0 + n], start=(kc == 0), stop=(kc == 3)),
                         reads=["wot", ("attnT", kc)], writes=[pk], signal=(kc == 3))
                P.op("dve", lambda e, dc=dc, c0=c0, n=n, pt=pt: e.tensor_tensor(out=x[:, dc, c0:c0 + n], in0=pt[:, 0:n], in1=x[:, dc, c0:c0 + n], op=ALU.add),
                     reads=[pk, ("x", dc, ti)], writes=[("x", dc, ti)])


    def run_chunk(ci, sample):
        T = NS if sample else TC
        P.barrier(keep=CONST + ["hprev0", "hprev1"])
        if sample:
            P.op("sp", lambda e: e.dma_start(out=x[:, :, 0:NS], in_=xsT.rearrange("(k p) t -> p k t", p=128)), writes=["xload"], dma=True)
        else:
            for k in range(8):
                P.op("sp", lambda e, k=k: e.dma_start(out=x[:, k, :], in_=xT[k * 128:(k + 1) * 128, ci * TC:(ci + 1) * TC]), writes=["xload"], dma=True)
        P.barrier(keep=CONST + ["hprev0", "hprev1"])
        for l in range(DBG_LAYERS):
            if l % 2 == 0:
                pool_layer(T, l, ci, sample)
            else:
                attn_layer(T, l, ci, sample)
            if not (DBG_SKIP_LAST_MLP and l == DBG_LAYERS - 1):
                mlp(T, l)
        P.barrier(keep=CONST + ["hprev0", "hprev1"])
        if sample:
            P.op("sp", lambda e: e.dma_start(out=ysT.rearrange("(k p) t -> p k t", p=128), in_=x[:, :, 0:NS]), writes=["OUT"], dma=True)
        else:
            for k in range(8):
                P.op("sp", lambda e, k=k: e.dma_start(out=yT[k * 128:(k + 1) * 128, ci * TC:(ci + 1) * TC], in_=x[:, k, :]), writes=["OUT"], dma=True)

    for ci in range(DBG_CHUNKS):
        run_chunk(ci, False)
    if DBG_SAMPLE:
        run_chunk(0, True)
    P.barrier(final=True)

    with nc.Block() as block:
        @block.tensor
        def _(e):
            for f in P.q["pe"]:
                f(e)

        @block.scalar
        def _(e):
            for f in P.q["act"]:
                f(e)

        @block.vector
        def _(e):
            for f in P.q["dve"]:
                f(e)

        @block.gpsimd
        def _(e):
            for f in P.q["pool"]:
                f(e)

        @block.sync
        def _(e):
            for f in P.q["sp"]:
                f(e)
    return nc


def _rope_tables():
    half = 32
    inv = (10000.0 ** (-np.arange(half, dtype=np.float32) * 2.0 / 64)).astype(np.float32)
    pos = np.concatenate([np.arange(SEQ), np.full(NS, SEQ)]).astype(np.float32)
    ang = pos[:, None] * inv[None, :]
    return np.cos(ang).astype(np.float32), np.sin(ang).astype(np.float32)


def kernel(x_prompt, x_sample, state_pool, cache_kv_w128, cache_kv_w512, cache_kv_w2048,
           norm_mix, norm_mlp, pool_w, pool_scale, attn_w_qkv, attn_q_norm, attn_k_norm,
           attn_w_o, mlp_w_up, mlp_w_down):
    f = lambda a: np.ascontiguousarray(np.asarray(a, dtype=np.float32))
    x_prompt, x_sample, state_pool = f(x_prompt), f(x_sample), f(state_pool)
    caches = [f(cache_kv_w128), f(cache_kv_w512), f(cache_kv_w2048)]
    cosT, sinT = _rope_tables()
    ii = np.arange(128)[:, None]
    jj = np.arange(128)[None, :]
    masks = np.concatenate([(jj <= ii), (jj >= ii)], axis=1).astype(np.float32).astype(ml_dtypes.bfloat16)
    invc = np.zeros((128, 8, 16), np.float32)
    for k in range(8):
        w = WIN[k // 2]
        invc[:, k, :] = 1.0 / np.minimum(np.arange(16) + 1, w).astype(np.float32)
    lay = lambda g: f(np.asarray(g, np.float32).reshape(g.shape[0], 8, 128).transpose(2, 0, 1))
    qkn = np.zeros((2, 3, 2, 128, 512), np.float32)
    qn_, kn_ = f(attn_q_norm), f(attn_k_norm)
    for li in range(2):
        for g in range(3):
            qkn[li, g, 0] = np.tile(qn_[li, g], (128, 8))
            qkn[li, g, 1] = np.tile(kn_[li, g], (128, 8))
    shared = {
        "gmix": lay(f(norm_mix)), "gmlp": lay(f(norm_mlp)), "pscale": lay(f(pool_scale)),
        "pool_w": f(pool_w), "wqkv": f(attn_w_qkv), "wo": f(attn_w_o), "wup": f(mlp_w_up), "wdn": f(mlp_w_down),
        "qkn": qkn, "sintab": sinT, "costab2": np.ascontiguousarray(np.concatenate([cosT, cosT], axis=1)), "nsintab": np.ascontiguousarray(-sinT), "masks": masks, "invc": invc,
        "identb": np.eye(128, dtype=np.float32).astype(ml_dtypes.bfloat16), "identf": np.eye(128, dtype=np.float32),
    }
    in_maps = []
    for c in range(NCORES):
        m = dict(shared)
        m["xT"] = f(x_prompt[c].T) if c < 2 else np.zeros((D, SEQ), np.float32)
        sl = slice(NS * c, NS * (c + 1))
        m["xsT"] = f(x_sample[sl, 0, :].T)
        m["stT"] = f(state_pool[:, sl].transpose(0, 3, 1, 2))
        m["st"] = f(state_pool[:, sl])
        m["c128"] = f(caches[0][:, sl].reshape(2, NS, 128, 2, 512))
        m["c512"] = f(caches[1][:, sl].reshape(2, NS, 512, 2, 512))
        m["c2048"] = f(caches[2][:, sl].reshape(2, NS, 2048, 2, 512))
        in_maps.append(m)
    nc = build_nc()
    res = run_bass_kernel_spmd(nc, in_maps, core_ids=list(range(NCORES))).results
    if DBG_DUMP:
        global LAST_RES
        LAST_RES = res
    y_prompt = np.stack([res[b]["yT"].T for b in range(2)]).astype(np.float32)
    y_sample = np.concatenate([res[c]["ysT"].T for c in range(NCORES)], axis=0)[:, None, :].astype(np.float32)
    pool_p = np.stack([res[b]["poolp"] for b in range(2)], axis=1).astype(np.float32)
    pool_s = np.concatenate([res[c]["pools"] for c in range(NCORES)], axis=1).astype(np.float32)
    outs = [y_prompt, y_sample, pool_p, pool_s]
    for g, (nm, nb) in enumerate((("128", 128), ("512", 512), ("2048", 2048))):
        kp = np.stack([res[b]["kp" + nm] for b in range(2)], axis=1).reshape(2, 2, nb, 2, 8, 64)
        ks = np.concatenate([res[c]["ks" + nm] for c in range(NCORES)], axis=1).reshape(2, NCORES * NS, nb, 2, 8, 64)
        outs += [kp.astype(np.float32), ks.astype(np.float32)]
    return tuple(np.ascontiguousarray(o) for o in outs)
```

```python
import numpy as np
import ml_dtypes
import concourse.bass as bass
import concourse.mybir as mybir
from concourse.bass_utils import run_bass_kernel_spmd

F32 = mybir.dt.float32
BF16 = mybir.dt.bfloat16
AF = mybir.ActivationFunctionType
ALU = mybir.AluOpType
AX = mybir.AxisListType

NCORES = 8
D = 1024
FF = 4096
SEQ = 8192
TC = 2048
NS = 4
EPS = 1e-6
DIL = [1, 4, 16]
NBUF = [128, 512, 2048]
WIN = [2, 4, 8, 16]
ENGS = ["pe", "act", "dve", "pool", "sp"]
import os
DBG_LAYERS = int(os.environ.get("KDBG_LAYERS", "4"))
DBG_SKIP_LAST_MLP = int(os.environ.get("KDBG_SKIPMLP", "0"))
DBG_CHUNKS = int(os.environ.get("KDBG_CHUNKS", "4"))
DBG_SAMPLE = int(os.environ.get("KDBG_SAMPLE", "1"))
DBG_DUMP = int(os.environ.get("KDBG_DUMP", "0"))
EPOCH = 8000


def ap_bcast_last(a, n):
    return bass.AP(a.tensor, a.offset, [list(x) for x in a.ap] + [[0, n]])


def ap_bcast_mid(a, n):
    l = [list(x) for x in a.ap]
    return bass.AP(a.tensor, a.offset, [l[0], [0, n]] + l[1:])


class Prog:
    def __init__(self, nc):
        self.nc = nc
        self.q = {e: [] for e in ENGS}
        self.cnt = {e: 0 for e in ENGS}
        self.waited = {e: {} for e in ENGS}
        self.lastw = {}
        self.readers = {}
        self.dmacnt = {}
        self.sems = {}
        self.pend_r = {e: set() for e in ENGS}
        self.pend_w = {e: set() for e in ENGS}
        self.last_tok = {e: None for e in ENGS}

    def sem(self, key):
        if key not in self.sems:
            self.sems[key] = self.nc.alloc_semaphore(name="s%d" % len(self.sems))
        return self.sems[key]

    def _need(self, eng, tok, is_dma):
        if tok is None:
            return None
        key, val, teng, tdma = tok
        if (not is_dma) and (not tdma) and teng == eng and eng == "pe":
            return None
        if self.waited[eng].get(key, 0) >= val:
            return None
        self.waited[eng][key] = val
        return (key, val)

    def op(self, eng, fn, reads=(), writes=(), signal=True, dma=False, nowaw=False):
        waits = []
        for b in reads:
            w = self._need(eng, self.lastw.get(b), dma)
            if w:
                waits.append(w)
        for b in writes:
            if nowaw:
                continue
            w = self._need(eng, self.lastw.get(b), dma)
            if w:
                waits.append(w)
            for t in self.readers.get(b, ()):
                w = self._need(eng, t, dma)
                if w:
                    waits.append(w)
        tok = None
        if dma:
            assert len(writes) == 1
            wb = writes[0]
            self.dmacnt[wb] = self.dmacnt.get(wb, 0) + 1
            tok = (("dma", wb), 16 * self.dmacnt[wb], eng, True)
        elif signal:
            self.cnt[eng] += 1
            ep = (self.cnt[eng] - 1) // EPOCH
            tok = (("eng", eng, ep), self.cnt[eng] - ep * EPOCH, eng, False)
            self.last_tok[eng] = tok
        wl = [(self.sem(k), v) for k, v in waits]
        ts = (self.sem(tok[0]), 16 if dma else 1) if tok else None

        def emit(e, wl=wl, fn=fn, ts=ts):
            for s, v in wl:
                e.wait_ge(s, v)
            ins = fn(e)
            if ts is not None:
                ins.then_inc(ts[0], ts[1])
        self.q[eng].append(emit)
        if tok is None:
            self.pend_r[eng].update(reads)
            self.pend_w[eng].update(writes)
        else:
            rs = set(reads)
            ws = set(writes)
            if not dma:
                rs |= self.pend_r[eng]
                ws |= self.pend_w[eng]
                self.pend_r[eng] = set()
                self.pend_w[eng] = set()
            for b in ws:
                self.lastw[b] = tok
                self.readers[b] = []
            for b in rs:
                if b not in ws:
                    self.readers.setdefault(b, []).append(tok)

    def barrier(self, keep=(), final=False):
        toks = {}
        for e in ENGS:
            t = self.last_tok[e]
            if t is not None:
                toks[t[0]] = max(toks.get(t[0], (0,))[0], t[1]), t
        for b, t in self.lastw.items():
            if t[3] and (final or b != "OUTC"):
                toks[t[0]] = max(toks.get(t[0], (0,))[0], t[1]), t
        for b, l in self.readers.items():
            for t in l:
                if t[3]:
                    toks[t[0]] = max(toks.get(t[0], (0,))[0], t[1]), t
        for e in ENGS:
            wl = []
            for key, (val, t) in toks.items():
                if (not t[3]) and t[2] == e:
                    continue
                if self.waited[e].get(key, 0) >= val:
                    continue
                self.waited[e][key] = val
                wl.append((self.sem(key), val))
            if wl:
                def emit(en, wl=wl):
                    for s, v in wl:
                        en.wait_ge(s, v)
                self.q[e].append(emit)
        keepw = {b: self.lastw[b] for b in list(keep) + ["OUTC"] if b in self.lastw}
        self.lastw = dict(keepw)
        self.readers = {}

    def finish(self, outbufs):
        wl = []
        for b in outbufs:
            t = self.lastw.get(b)
            if t is not None:
                wl.append((self.sem(t[0]), t[1]))

        def emit(e, wl=wl):
            for s, v in wl:
                e.wait_ge(s, v)
        self.q["sp"].append(emit)


def build_nc():
    nc = bass.Bass("TRN2", target_bir_lowering=False)
    P = Prog(nc)

    def din(name, shape, dt=F32):
        return nc.dram_tensor(name, list(shape), dt, kind="ExternalInput").ap()

    def dout(name, shape):
        return nc.dram_tensor(name, list(shape), F32, kind="ExternalOutput").ap()

    def dscr(name, shape, dt):
        if DBG_DUMP and name[2] == "0":
            return nc.dram_tensor(name, list(shape), dt, kind="ExternalOutput").ap()
        return nc.dram_tensor(name, list(shape), dt).ap()
    dbg_attn = nc.dram_tensor("dbg_attn", [128, 4, TC], BF16, kind="ExternalOutput").ap() if DBG_DUMP else None

    xT = din("xT", [D, SEQ])
    xsT = din("xsT", [D, NS])
    stT = din("stT", [2, D, NS, 15])
    st = din("st", [2, NS, 15, D])
    cch = [din("c128", [2, NS, 128, 2, 512]), din("c512", [2, NS, 512, 2, 512]), din("c2048", [2, NS, 2048, 2, 512])]
    gmix = din("gmix", [128, 4, 8])
    gmlp = din("gmlp", [128, 4, 8])
    pscale = din("pscale", [128, 2, 8])
    pool_w = din("pool_w", [2, 4, 256, 256])
    wqkv = din("wqkv", [2, D, 4608])
    wo = din("wo", [2, 512, D])
    wup = din("wup", [4, D, FF])
    wdn = din("wdn", [4, FF, D])
    qkn = din("qkn", [2, 3, 2, 128, 512])
    sintab = din("sintab", [SEQ + NS, 32])
    costab2 = din("costab2", [SEQ + NS, 64])
    nsintab = din("nsintab", [SEQ + NS, 32])
    masks_d = din("masks", [128, 256], BF16)
    invc_d = din("invc", [128, 8, 16])
    identb_d = din("identb", [128, 128], BF16)
    identf_d = din("identf", [128, 128])

    yT = dout("yT", [D, SEQ])
    ysT = dout("ysT", [D, NS])
    poolp = dout("poolp", [2, 15, D])
    pools = dout("pools", [2, NS, 15, D])
    kvp = [dout("kp128", [2, 128, 2, 512]), dout("kp512", [2, 512, 2, 512]), dout("kp2048", [2, 2048, 2, 512])]
    kvs = [dout("ks128", [2, NS, 128, 2, 512]), dout("ks512", [2, NS, 512, 2, 512]), dout("ks2048", [2, NS, 2048, 2, 512])]

    Ks = [[dscr("Ks%d%d" % (li, g), [SEQ + 16, 512], BF16) for g in range(3)] for li in range(2)]
    Vs = [[dscr("Vs%d%d" % (li, g), [SEQ + 16, 512], BF16) for g in range(3)] for li in range(2)]
    Qs = [[dscr("Qs%d%d" % (li, g), [SEQ + 16, 512], BF16) for g in range(3)] for li in range(2)]

    wup_b = dscr("wupb", [4, D, FF], BF16)
    wdn_b = dscr("wdnb", [4, FF, D], BF16)
    wqkv_b = dscr("wqkvb", [2, D, 4608], BF16)
    wo_b = dscr("wob", [2, 512, D], BF16)
    poolw_b = dscr("poolwb", [2, 4, 256, 256], BF16)

    def convert(src, dst, pat, kw, nsplit, key):
        sv = src.rearrange(pat, **kw)
        dv = dst.rearrange(pat, **kw)
        rows = sv.shape[0]
        step = rows // nsplit
        for i in range(nsplit):
            P.op("pool", lambda e, sv=sv, dv=dv, i=i, step=step: e.dma_start(out=dv[i * step:(i + 1) * step, :], in_=sv[i * step:(i + 1) * step, :]),
                 writes=[key], dma=True)

    for l in range(4):
        convert(wup[l], wup_b[l], "d (a f) -> (d a) f", dict(f=2048), 4, "cv_up")
        convert(wdn[l], wdn_b[l], "(r a) d -> r (a d)", dict(a=2), 4, "cv_dn")
    for li in range(2):
        convert(wqkv[li], wqkv_b[li], "d (a f) -> (d a) f", dict(f=1536), 4, "cv_qkv")
        convert(wo[li], wo_b[li], "r d -> r d", dict(), 1, "cv_o")
        convert(pool_w[li], poolw_b[li], "g c e -> (g c) e", dict(), 1, "cv_p")

    base = ((nc.sbuf_base + 63) // 64) * 64
    top = nc.sbuf_top
    state = {"off": base, "n": 0}

    def alloc(shape, dt):
        esz = 4 if dt == F32 else 2
        nb = esz
        for s in shape[1:]:
            nb *= s
        nb = ((nb + 63) // 64) * 64
        off = state["off"]
        assert off + nb <= top, ("SBUF overflow", off, nb, top)
        state["off"] = off + nb
        state["n"] += 1
        return nc.alloc_sbuf_tensor_at("t%d" % state["n"], list(shape), dt, offset=off)

    x = alloc([128, 8, TC], F32)
    h = alloc([128, 8, TC], BF16)
    ones_bf = alloc([128, 128], BF16)
    identb = alloc([128, 128], BF16)
    identf = alloc([128, 128], F32)
    masks = alloc([128, 256], BF16)
    invc = alloc([128, 8, 16], F32)
    g_mix = alloc([128, 4, 8], F32)
    g_mlp = alloc([128, 4, 8], F32)
    p_scale = alloc([128, 2, 8], F32)
    hprev = alloc([128, 2, 8, 15], F32)
    phase_base = state["off"]

    ps = [nc.alloc_psum_tensor("ps%d" % i, [128, 512], F32) for i in range(7)]
    psb = nc.alloc_psum_tensor("psb", [128, 1024], BF16)
    psrr = {"i": 0}

    def next_ps():
        i = psrr["i"] % 7
        psrr["i"] += 1
        return ps[i], ("ps", i)

    P.op("sp", lambda e: e.dma_start(out=identb[:], in_=identb_d), writes=["identb"], dma=True)
    P.op("sp", lambda e: e.dma_start(out=identf[:], in_=identf_d), writes=["identf"], dma=True)
    P.op("sp", lambda e: e.dma_start(out=masks[:], in_=masks_d), writes=["masks"], dma=True)
    P.op("sp", lambda e: e.dma_start(out=invc[:], in_=invc_d), writes=["invc"], dma=True)
    P.op("sp", lambda e: e.dma_start(out=g_mix[:], in_=gmix), writes=["g_mix"], dma=True)
    P.op("sp", lambda e: e.dma_start(out=g_mlp[:], in_=gmlp), writes=["g_mlp"], dma=True)
    P.op("sp", lambda e: e.dma_start(out=p_scale[:], in_=pscale), writes=["p_scale"], dma=True)
    P.op("dve", lambda e: e.memset(ones_bf[:], 1.0), writes=["ones"])
    for li_ in range(2):
        for g_ in range(3):
            for b_ in range(NS):
                P.op("sp", lambda e, g_=g_, li_=li_, b_=b_: e.dma_start(out=kvs[g_][li_, b_, 0:NBUF[g_] - 1, :, :], in_=cch[g_][li_, b_, 1:NBUF[g_], :, :]),
                     writes=["OUTC"], dma=True, nowaw=True)
    CONST = ["identb", "identf", "masks", "invc", "g_mix", "g_mlp", "p_scale", "ones"]

    def tiles_of(T):
        if T >= 512:
            return [(i * 512, 512) for i in range(T // 512)]
        return [(0, T)]

    def rmsnorm(T, gt, gkey, l, out_fn, sq, rs, extra_w):
        for ti, (c0, n) in enumerate(tiles_of(T)):
            for k in range(8):
                P.op("act", lambda e, k=k, c0=c0, n=n: e.activation(out=sq[:, k, 0:n], in_=x[:, k, c0:c0 + n], func=AF.Square),
                     reads=[("x", k, ti)], writes=[("sq", k)])
            pt, pk = next_ps()
            for k in range(8):
                P.op("pe", lambda e, k=k, n=n, pt=pt: e.matmul(pt[:, 0:n], lhsT=ones_bf[:], rhs=sq[:, k, 0:n], start=(k == 0), stop=(k == 7)),
                     reads=[("sq", k), "ones"], writes=[pk], signal=(k == 7))
            P.op("act", lambda e, n=n, pt=pt: e.activation(out=rs[:, 0:n], in_=pt[:, 0:n], func=AF.Sqrt, bias=EPS, scale=1.0 / D),
                 reads=[pk], writes=["rs"])
            P.op("dve", lambda e, n=n: e.reciprocal(out=rs[:, 0:n], in_=rs[:, 0:n]), reads=["rs"], writes=["rs"])
            for k in range(8):
                o = out_fn(k, c0, n)
                P.op("dve", lambda e, k=k, c0=c0, n=n, o=o: e.scalar_tensor_tensor(out=o, in0=x[:, k, c0:c0 + n], scalar=gt[:, l, k:k + 1], in1=rs[:, 0:n], op0=ALU.mult, op1=ALU.mult),
                     reads=[("x", k, ti), "rs", gkey], writes=[extra_w(k, ti)])

    def mlp(T, l):
        state["off"] = phase_base
        P.barrier(keep=CONST)
        sq = alloc([128, 8, 512], BF16)
        rs = alloc([128, 512], F32)
        rmsnorm(T, g_mlp, "g_mlp", l, lambda k, c0, n: h[:, k, c0:c0 + n], sq, rs, lambda k, ti: ("h", k, ti))
        hid = [alloc([128, 4, T], BF16) for _ in range(2)]
        NW = 3
        wu = [alloc([128, 8, 512], BF16) for _ in range(NW)]
        wd = [alloc([128, 4, D], BF16) for _ in range(NW)]
        NRL = 4
        rl = [alloc([128, 512], F32) for _ in range(NRL)]
        tl = tiles_of(T)
        rlc = {"i": 0}

        def load_w(fb):
            s = fb % NW
            P.op("sp", lambda e: e.dma_start(out=wu[s][:], in_=wup_b[l].rearrange("(k p) f -> p k f", p=128)[:, :, fb * 512:(fb + 1) * 512]),
                 writes=[("wu", s)], dma=True)
            P.op("sp", lambda e: e.dma_start(out=wd[s][:], in_=wdn_b[l, fb * 512:(fb + 1) * 512, :].rearrange("(k p) d -> p k d", p=128)),
                 writes=[("wd", s)], dma=True)

        def up(fb):
            s = fb % NW
            hs = fb % 2
            for fc in range(4):
                for ti, (c0, n) in enumerate(tl):
                    pt, pk = next_ps()
                    for k in range(8):
                        P.op("pe", lambda e, k=k, c0=c0, n=n, pt=pt, fc=fc: e.matmul(pt[:, 0:n], lhsT=wu[s][:, k, fc * 128:(fc + 1) * 128], rhs=h[:, k, c0:c0 + n], start=(k == 0), stop=(k == 7)),
                             reads=[("wu", s), ("h", k, ti)], writes=[pk], signal=(k == 7))
                    r = rlc["i"] % NRL
                    rlc["i"] += 1
                    P.op("act", lambda e, n=n, pt=pt, r=r: e.activation(out=rl[r][:, 0:n], in_=pt[:, 0:n], func=AF.Relu),
                         reads=[pk], writes=[("rl", r)])
                    P.op("pool", lambda e, n=n, c0=c0, r=r, fc=fc: e.tensor_tensor(out=hid[hs][:, fc, c0:c0 + n], in0=rl[r][:, 0:n], in1=rl[r][:, 0:n], op=ALU.mult),
                         reads=[("rl", r)], writes=[("hid", hs, fc, ti)])

        def down(fb):
            s = fb % NW
            hs = fb % 2
            for dc in range(8):
                for ti, (c0, n) in enumerate(tl):
                    pt, pk = next_ps()
                    for fc in range(4):
                        P.op("pe", lambda e, fc=fc, c0=c0, n=n, pt=pt, dc=dc: e.matmul(pt[:, 0:n], lhsT=wd[s][:, fc, dc * 128:(dc + 1) * 128], rhs=hid[hs][:, fc, c0:c0 + n], start=(fc == 0), stop=(fc == 3)),
                             reads=[("wd", s), ("hid", hs, fc, ti)], writes=[pk], signal=(fc == 3))
                    P.op("dve", lambda e, c0=c0, n=n, pt=pt, dc=dc: e.tensor_tensor(out=x[:, dc, c0:c0 + n], in0=pt[:, 0:n], in1=x[:, dc, c0:c0 + n], op=ALU.add),
                         reads=[pk, ("x", dc, ti)], writes=[("x", dc, ti)])

        load_w(0)
        load_w(1)
        up(0)
        for fb in range(8):
            if fb + 2 < 8:
                load_w(fb + 2)
            if fb + 1 < 8:
                up(fb + 1)
            down(fb)

    def pool_layer(T, l, ci, sample):
        li = l // 2
        state["off"] = phase_base
        P.barrier(keep=CONST + ["hprev0", "hprev1"])
        sq = alloc([128, 8, 512], BF16)
        rs = alloc([128, 512], F32)
        wp = alloc([128, 4, 2, 256], BF16)
        for gi in range(4):
            P.op("sp", lambda e, gi=gi: e.dma_start(out=wp[:, gi, :, :], in_=poolw_b[li, gi].rearrange("(c p) e -> p c e", p=128)), writes=["wp"], dma=True)
        if not sample:
            E = alloc([128, 8, 16 + 512], F32)
            W1 = alloc([128, 2, 16 + 512], F32)
            W2 = alloc([128, 2, 16 + 512], F32)
            tmp16 = alloc([128, 2, 16], F32)
            tl = tiles_of(T)
            for ti, (c0, n) in enumerate(tl):
                if ti == 0:
                    if ci == 0:
                        P.op("dve", lambda e: e.memset(E[:, :, 0:16], 0.0), writes=["Eh"])
                    else:
                        P.op("dve", lambda e: e.tensor_copy(out=E[:, :, 1:16], in_=hprev[:, li, :, :]), reads=["hprev%d" % li], writes=["Eh"])
                else:
                    P.op("dve", lambda e: e.tensor_copy(out=E[:, :, 1:16], in_=E[:, :, 513:528]), reads=["E"], writes=["Eh"])
                for k in range(8):
                    P.op("act", lambda e, k=k, c0=c0, n=n: e.activation(out=sq[:, k, 0:n], in_=x[:, k, c0:c0 + n], func=AF.Square),
                         reads=[("x", k, ti)], writes=[("sq", k)])
                pt, pk = next_ps()
                for k in range(8):
                    P.op("pe", lambda e, k=k, n=n, pt=pt: e.matmul(pt[:, 0:n], lhsT=ones_bf[:], rhs=sq[:, k, 0:n], start=(k == 0), stop=(k == 7)),
                         reads=[("sq", k), "ones"], writes=[pk], signal=(k == 7))
                P.op("act", lambda e, n=n, pt=pt: e.activation(out=rs[:, 0:n], in_=pt[:, 0:n], func=AF.Sqrt, bias=EPS, scale=1.0 / D),
                     reads=[pk], writes=["rs"])
                P.op("dve", lambda e, n=n: e.reciprocal(out=rs[:, 0:n], in_=rs[:, 0:n]), reads=["rs"], writes=["rs"])
                for k in range(8):
                    P.op("dve", lambda e, k=k, c0=c0, n=n: e.scalar_tensor_tensor(out=E[:, k, 16:16 + n], in0=x[:, k, c0:c0 + n], scalar=g_mix[:, l, k:k + 1], in1=rs[:, 0:n], op0=ALU.mult, op1=ALU.mult),
                         reads=[("x", k, ti), "rs", "g_mix", "Eh"], writes=["E"])
                L = 16 + n
                for gi in range(4):
                    w = WIN[gi]
                    k0 = 2 * gi
                    P.op("pool", lambda e, k0=k0, L=L: e.tensor_tensor(out=W1[:, :, 1:L], in0=E[:, k0:k0 + 2, 1:L], in1=E[:, k0:k0 + 2, 0:L - 1], op=ALU.add),
                         reads=["E", "Eh"], writes=["W1"])
                    cur = W1
                    ck = "W1"
                    if w >= 4:
                        P.op("pool", lambda e, L=L: e.tensor_tensor(out=W2[:, :, 3:L], in0=W1[:, :, 3:L], in1=W1[:, :, 1:L - 2], op=ALU.add),
                             reads=["W1"], writes=["W2"])
                        cur, ck = W2, "W2"
                    if w >= 8:
                        P.op("pool", lambda e, L=L: e.tensor_tensor(out=W1[:, :, 7:L], in0=W2[:, :, 7:L], in1=W2[:, :, 3:L - 4], op=ALU.add),
                             reads=["W2"], writes=["W1"])
                        cur, ck = W1, "W1"
                    if w >= 16:
                        P.op("pool", lambda e, L=L: e.tensor_tensor(out=W2[:, :, 15:L], in0=W1[:, :, 15:L], in1=W1[:, :, 7:L - 8], op=ALU.add),
                             reads=["W1"], writes=["W2"])
                        cur, ck = W2, "W2"
                    P.op("dve", lambda e, cur=cur, k0=k0, c0=c0, n=n, w=w: e.scalar_tensor_tensor(out=h[:, k0:k0 + 2, c0:c0 + n], in0=cur[:, :, 16:16 + n], scalar=1.0 / w, in1=E[:, k0:k0 + 2, 16:16 + n], op0=ALU.mult, op1=ALU.subtract),
                         reads=[ck, "E"], writes=[("h", k0, ti), ("h", k0 + 1, ti)])
                    if ci == 0 and ti == 0:
                        P.op("dve", lambda e, cur=cur, k0=k0: e.tensor_tensor(out=tmp16[:], in0=cur[:, :, 16:32], in1=invc[:, k0:k0 + 2, :], op=ALU.mult),
                             reads=[ck, "invc"], writes=["tmp16"])
                        P.op("dve", lambda e, k0=k0: e.tensor_tensor(out=h[:, k0:k0 + 2, 0:16], in0=tmp16[:], in1=E[:, k0:k0 + 2, 16:32], op=ALU.subtract),
                             reads=["tmp16", "E"], writes=[("h", k0, ti), ("h", k0 + 1, ti)])
                if ti == len(tl) - 1:
                    P.op("dve", lambda e, n=n: e.tensor_copy(out=hprev[:, li, :, :], in_=E[:, :, 16 + n - 15:16 + n]), reads=["E"], writes=["hprev%d" % li])
                    if ci == 3:
                        rows = alloc([15, D], F32)
                        for hf in range(2):
                            pt, pk = next_ps()
                            for k in range(4):
                                P.op("pe", lambda e, k=k, n=n, pt=pt, hf=hf: e.transpose(pt[0:15, k * 128:(k + 1) * 128], E[:, hf * 4 + k, 16 + n - 15:16 + n], identf[:]),
                                     reads=["E", "identf"], writes=[pk], signal=(k == 3))
                            P.op("dve", lambda e, pt=pt, hf=hf: e.tensor_copy(out=rows[:, hf * 512:(hf + 1) * 512], in_=pt[0:15, :]), reads=[pk], writes=["prow"])
                        P.op("sp", lambda e: e.dma_start(out=poolp[li], in_=rows[:]), reads=["prow"], writes=["OUT"], dma=True)
                pool_matmul(wp, T, li, [(ti, c0, n)])
        else:
            Es = alloc([128, 8, NS, 16], F32)
            ssum = alloc([128, 2, NS], F32)
            rows = alloc([NS, D], F32)
            for k in range(8):
                P.op("sp", lambda e, k=k: e.dma_start(out=Es[:, k, :, 0:15], in_=stT[li, k * 128:(k + 1) * 128, :, :]), writes=["Es"], dma=True)
            rmsnorm(T, g_mix, "g_mix", l, lambda k, c0, n: Es[:, k, :, 15], sq, rs, lambda k, ti: "Es")
            for gi in range(4):
                w = WIN[gi]
                k0 = 2 * gi
                P.op("dve", lambda e, k0=k0, w=w: e.tensor_reduce(out=ssum[:], in_=Es[:, k0:k0 + 2, :, 16 - w:16], axis=AX.X, op=ALU.add),
                     reads=["Es"], writes=["ssum"])
                P.op("dve", lambda e, k0=k0, w=w: e.scalar_tensor_tensor(out=h[:, k0:k0 + 2, 0:NS], in0=ssum[:], scalar=1.0 / w, in1=Es[:, k0:k0 + 2, :, 15], op0=ALU.mult, op1=ALU.subtract),
                     reads=["ssum", "Es"], writes=[("h", k0, 0), ("h", k0 + 1, 0)])
            P.op("sp", lambda e: e.dma_start(out=pools[li, :, 0:14, :], in_=st[li, :, 1:15, :]), writes=["OUT"], dma=True)
            for hf in range(2):
                pt, pk = next_ps()
                for k in range(4):
                    P.op("pe", lambda e, k=k, pt=pt, hf=hf: e.transpose(pt[0:NS, k * 128:(k + 1) * 128], Es[:, hf * 4 + k, :, 15], identf[:]),
                         reads=["Es", "identf"], writes=[pk], signal=(k == 3))
                P.op("dve", lambda e, pt=pt, hf=hf: e.tensor_copy(out=rows[:, hf * 512:(hf + 1) * 512], in_=pt[0:NS, :]), reads=[pk], writes=["prow"])
            P.op("sp", lambda e: e.dma_start(out=pools[li, :, 14, :], in_=rows[:]), reads=["prow"], writes=["OUT"], dma=True)
            pool_matmul(wp, T, li, [(0, 0, NS)])

    def pool_matmul(wp, T, li, tl):
        for ti, c0, n in tl:
            for gi in range(4):
                for ec in range(2):
                    pt, pk = next_ps()
                    for cc in range(2):
                        P.op("pe", lambda e, gi=gi, ec=ec, cc=cc, c0=c0, n=n, pt=pt: e.matmul(pt[:, 0:n], lhsT=wp[:, gi, cc, ec * 128:(ec + 1) * 128], rhs=h[:, 2 * gi + cc, c0:c0 + n], start=(cc == 0), stop=(cc == 1)),
                             reads=["wp", ("h", 2 * gi + cc, ti)], writes=[pk], signal=(cc == 1))
                    kk = 2 * gi + ec
                    P.op("dve", lambda e, kk=kk, c0=c0, n=n, pt=pt: e.scalar_tensor_tensor(out=x[:, kk, c0:c0 + n], in0=pt[:, 0:n], scalar=p_scale[:, li, kk:kk + 1], in1=x[:, kk, c0:c0 + n], op0=ALU.mult, op1=ALU.add),
                         reads=[pk, ("x", kk, ti), "p_scale"], writes=[("x", kk, ti)])

    def attn_layer(T, l, ci, sample):
        li = l // 2
        state["off"] = phase_base
        P.barrier(keep=CONST)
        sq = alloc([128, 8, 512], BF16)
        rs = alloc([128, 512], F32)
        rmsnorm(T, g_mix, "g_mix", l, lambda k, c0, n: h[:, k, c0:c0 + n], sq, rs, lambda k, ti: ("h", k, ti))
        row0 = SEQ if sample else ci * TC
        nblk = 1 if sample else T // 128
        npb = NS if sample else 128
        wq = [alloc([128, 8, 512], BF16) for _ in range(2)]
        gn = [alloc([128, 512], F32) for _ in range(2)]
        cs = alloc([128, 16, 32], F32)
        sn = alloc([128, 16, 32], F32)
        qf = [alloc([128, 512], F32) for _ in range(4)]
        qn = [alloc([128, 512], F32) for _ in range(4)]
        sqf2 = [alloc([128, 512], F32) for _ in range(4)]
        ssh2 = [alloc([128, 8], F32) for _ in range(4)]
        t1a2 = [alloc([128, 8, 32], F32) for _ in range(4)]
        t2a2 = [alloc([128, 8, 32], F32) for _ in range(4)]
        t1b2 = [alloc([128, 8, 32], F32) for _ in range(4)]
        t2b2 = [alloc([128, 8, 32], F32) for _ in range(4)]
        sqf = sqf2[0]
        qb16 = [alloc([128, 512], BF16) for _ in range(4)]
        ssh = alloc([128, 8], F32)
        t1 = alloc([128, 8, 32], F32)
        t2 = alloc([128, 8, 32], F32)
        cs2 = alloc([128, 16, 64], F32)
        nsn = alloc([128, 16, 32], F32)
        t2f = [alloc([128, 512], F32) for _ in range(4)]
        P.op("sp", lambda e: e.dma_start(out=cs2[0:npb, 0:nblk, :], in_=costab2[row0:row0 + npb * nblk, :].rearrange("(b p) c -> p b c", p=npb)), writes=["cs2"], dma=True)
        P.op("sp", lambda e: e.dma_start(out=nsn[0:npb, 0:nblk, :], in_=nsintab[row0:row0 + npb * nblk, :].rearrange("(b p) c -> p b c", p=npb)), writes=["nsn"], dma=True)
        P.op("sp", lambda e: e.dma_start(out=sn[0:npb, 0:nblk, :], in_=sintab[row0:row0 + npb * nblk, :].rearrange("(b p) c -> p b c", p=npb)), writes=["sn"], dma=True)
        NBF = 4
        ptmap = {}

        def S1(tb, s, part, g):
            pt, pk = next_ps()
            for k in range(8):
                P.op("pe", lambda e, k=k: e.matmul(pt[0:npb, :], lhsT=h[:, k, tb * npb:(tb + 1) * npb], rhs=wq[s][:, k, :], start=(k == 0), stop=(k == 7)),
                     reads=[("wq", s), ("h", k, tb // 4)], writes=[pk], signal=(k == 7))
            r = tb % NBF
            of = qf[r]
            if part == 2:
                P.op("act", lambda e: e.activation(out=of[0:npb, :], in_=pt[0:npb, :], func=AF.Copy), reads=[pk], writes=[("qf", r)])
                return
            sqf_, ssh_, qq = sqf2[r], ssh2[r], qn[r]
            ptmap[tb] = (pt, pk)
            P.op("act", lambda e: e.activation(out=sqf_[0:npb, :], in_=pt[0:npb, :], func=AF.Square), reads=[pk], writes=[("sqf", r)])
            P.op("dve", lambda e: e.tensor_reduce(out=ssh_[0:npb, :], in_=sqf_[0:npb, :].rearrange("p (h d) -> p h d", d=64), axis=AX.X, op=ALU.add), reads=[("sqf", r)], writes=[("ssh", r)])

        def S1b(tb, s, part, g):
            if part == 2:
                return
            r = tb % NBF
            ssh_, qq = ssh2[r], qn[r]
            pt, pk = ptmap[tb]
            P.op("act", lambda e: e.activation(out=ssh_[0:npb, :], in_=ssh_[0:npb, :], func=AF.Sqrt, bias=EPS, scale=1.0 / 64), reads=[("ssh", r)], writes=[("ssh", r)])
            P.op("dve", lambda e: e.reciprocal(out=ssh_[0:npb, :], in_=ssh_[0:npb, :]), reads=[("ssh", r)], writes=[("ssh", r)])
            P.op("dve", lambda e: e.tensor_tensor(out=qq[0:npb, :].rearrange("p (h d) -> p h d", d=64), in0=pt[0:npb, :].rearrange("p (h d) -> p h d", d=64), in1=ap_bcast_last(ssh_[0:npb, :], 64), op=ALU.mult),
                 reads=[pk, ("ssh", r)], writes=[("qn", r)])
            P.op("pool", lambda e: e.tensor_tensor(out=qq[0:npb, :], in0=qq[0:npb, :], in1=gn[s][0:npb, :], op=ALU.mult),
                 reads=[("qn", r), ("gn", s)], writes=[("qn", r)])

        def S2(tb, s, part, g):
            if part == 2:
                return
            r = tb % NBF
            qq = qn[r]
            t1f, t2f_ = sqf2[r], t2f[r]
            q3 = qq[0:npb, :].rearrange("p (h d) -> p h d", d=64)
            t13 = t1f[0:npb, :].rearrange("p (h d) -> p h d", d=64)
            t23 = t2f_[0:npb, :].rearrange("p (h d) -> p h d", d=64)
            cb = ap_bcast_mid(cs2[0:npb, tb, :], 8)
            sb = ap_bcast_mid(sn[0:npb, tb, :], 8)
            nsb = ap_bcast_mid(nsn[0:npb, tb, :], 8)
            P.op("dve", lambda e: e.tensor_tensor(out=t13, in0=q3, in1=cb, op=ALU.mult), reads=[("qn", r), "cs2"], writes=[("sqf", r)])
            P.op("pool", lambda e: e.tensor_tensor(out=t23[:, :, 0:32], in0=q3[:, :, 32:64], in1=nsb, op=ALU.mult), reads=[("qn", r), "nsn"], writes=[("t2f", r)])
            P.op("pool", lambda e: e.tensor_tensor(out=t23[:, :, 32:64], in0=q3[:, :, 0:32], in1=sb, op=ALU.mult), reads=[("qn", r), "sn"], writes=[("t2f", r)])

        def S3(tb, s, part, g):
            r = tb % NBF
            of = qf[r]
            if part < 2:
                t1f, t2f_ = sqf2[r], t2f[r]
                P.op("dve", lambda e: e.tensor_tensor(out=of[0:npb, :], in0=t1f[0:npb, :], in1=t2f_[0:npb, :], op=ALU.add), reads=[("sqf", r), ("t2f", r)], writes=[("qf", r)])
            scr = [Qs, Ks, Vs][part][li][g]
            r0 = row0 + tb * npb
            ob = qb16[r]
            P.op("act", lambda e: e.activation(out=ob[0:npb, :], in_=of[0:npb, :], func=AF.Copy),
                 reads=[("qf", r)], writes=[("qb16", r)])
            P.op("sp", lambda e: e.dma_start(out=scr[r0:r0 + npb, :], in_=ob[0:npb, :]),
                 reads=[("qb16", r)], writes=[("scr", part, g)], dma=True)
            if part >= 1:
                nb = NBUF[g]
                if sample:
                    P.op("sp", lambda e: e.dma_start(out=kvs[g][li, :, nb - 1, part - 1, :], in_=of[0:NS, :]),
                         reads=[("qf", r)], writes=["OUT"], dma=True)
                else:
                    pos0 = ci * TC + tb * 128
                    if pos0 >= SEQ - nb:
                        rr = pos0 - (SEQ - nb)
                        P.op("sp", lambda e: e.dma_start(out=kvp[g][li, rr:rr + 128, part - 1, :], in_=of[:, :]),
                             reads=[("qf", r)], writes=["OUT"], dma=True)

        def load_combo(cidx):
            g_, part_ = cidx // 3, cidx % 3
            s_ = cidx % 2
            col0_ = part_ * 1536 + g_ * 512
            P.op("sp", lambda e: e.dma_start(out=wq[s_][:], in_=wqkv_b[li].rearrange("(k p) f -> p k f", p=128)[:, :, col0_:col0_ + 512]),
                 writes=[("wq", s_)], dma=True)
            if part_ < 2:
                P.op("sp", lambda e: e.dma_start(out=gn[s_][:], in_=qkn[li, g_, part_]), writes=[("gn", s_)], dma=True)

        load_combo(0)
        combo = 0
        for g in range(3):
            for part in range(3):
                s = combo % 2
                combo += 1
                if combo < 9:
                    load_combo(combo)
                for t in range(nblk + 3):
                    if 0 <= t - 3 < nblk:
                        S3(t - 3, s, part, g)
                    if 0 <= t - 2 < nblk:
                        S2(t - 2, s, part, g)
                    if 0 <= t - 1 < nblk:
                        S1b(t - 1, s, part, g)
                    if t < nblk:
                        S1(t, s, part, g)
        state["off"] = phase_base
        P.barrier(keep=CONST)
        accOL = alloc([128, 2, T], F32)
        accO = accOL[:, 0, :]
        accL = accOL[:, 1, :]
        attnT = alloc([128, 4, T], BF16)
        onesc = ones_bf
        NB1 = 32
        qrows2 = [alloc([128, 16, 128], BF16) for _ in range(2)]
        krows2 = [alloc([128, NB1, 128], BF16) for _ in range(2)]
        vrows2 = [alloc([128, NB1, 128], BF16) for _ in range(2)]
        qT2 = [alloc([128, 16 * 128], BF16) for _ in range(2)]
        kT2 = [alloc([128, NB1 * 128], BF16) for _ in range(2)]
        qrows, krows, vrows, qT, kT = qrows2[0], krows2[0], vrows2[0], qT2[0], kT2[0]
        NPT = 6
        pT = [alloc([128, 256], BF16) for _ in range(NPT)]
        knew = alloc([NS, 128], BF16)
        vnew = alloc([1, NS, 128], BF16)
        scale = 64 ** -0.5
        clsn = {"i": 0, "it": 0}

        def acc_keys(nm, g, hh, qb):
            if g == 0:
                return [(nm, hh, qb // 4)]
            if g == 1:
                return [(nm, hh, qb)]
            return [(nm, hh, t) for t in range(4)]

        def class_setup(pc, g, cs_):
            Dg = DIL[g]
            nqb = (T // Dg) // 128
            first = ci * TC
            halo = (ci > 0)
            hoff = 1 if halo else 0
            nkc = nqb + hoff
            qr, kr, vr, qt, kt = qrows2[cs_], krows2[cs_], vrows2[cs_], qT2[cs_], kT2[cs_]
            if Dg == 1:
                kstart = first - (128 if halo else 0)
                P.op("sp", lambda e: e.dma_start(out=qr[:, 0:nqb, :], in_=Qs[li][g][first:first + 128 * nqb, pc].rearrange("(b p) c -> p b c", p=128)),
                     reads=[("scr", 0, g)], writes=[("qrows", cs_)], dma=True)
                P.op("sp", lambda e: e.dma_start(out=kr[:, 0:nkc, :], in_=Ks[li][g][kstart:kstart + 128 * nkc, pc].rearrange("(b p) c -> p b c", p=128)),
                     reads=[("scr", 1, g)], writes=[("krows", cs_)], dma=True)
                P.op("sp", lambda e: e.dma_start(out=vr[:, 0:nkc, :], in_=Vs[li][g][kstart:kstart + 128 * nkc, pc].rearrange("(b p) c -> p b c", p=128)),
                     reads=[("scr", 2, g)], writes=[("vrows", cs_)], dma=True)
            else:
                span = 128 * Dg
                for bb in range(nqb):
                    r0 = first + span * bb
                    P.op("sp", lambda e, bb=bb, r0=r0: e.dma_start(out=qr[:, 0:Dg * nqb, :].rearrange("p (r b) c -> p b r c", b=nqb)[:, bb, :, :], in_=Qs[li][g][r0:r0 + span, pc].rearrange("(p r) c -> p r c", r=Dg)),
                         reads=[("scr", 0, g)], writes=[("qrows", cs_)], dma=True)
                for bb in range(-hoff, nqb):
                    r0 = first + span * bb
                    for (scr_, t_, key_, part_) in ((Ks, kr, "krows", 1), (Vs, vr, "vrows", 2)):
                        P.op("sp", lambda e, bb=bb, r0=r0, scr_=scr_, t_=t_: e.dma_start(out=t_[:, 0:Dg * nkc, :].rearrange("p (r k) c -> p k r c", k=nkc)[:, bb + hoff, :, :], in_=scr_[li][g][r0:r0 + span, pc].rearrange("(p r) c -> p r c", r=Dg)),
                             reads=[("scr", part_, g)], writes=[(key_, cs_)], dma=True)
            for (src, skey, dst, dkey, nbk) in ((qr, ("qrows", cs_), qt, ("qT", cs_), Dg * nqb), (kr, ("krows", cs_), kt, ("kT", cs_), Dg * nkc)):
                for b0 in range(0, nbk, 8):
                    nn = min(8, nbk - b0)
                    for j in range(nn):
                        P.op("pe", lambda e, src=src, b0=b0, j=j: e.transpose(psb[:, j * 128:(j + 1) * 128], src[:, b0 + j, :], identb[:]),
                             reads=[skey, "identb"], writes=["psb"], signal=(j == nn - 1))
                    P.op("act", lambda e, dst=dst, b0=b0, nn=nn: e.activation(out=dst[:, b0 * 128:(b0 + nn) * 128], in_=psb[:, 0:nn * 128], func=AF.Copy),
                         reads=["psb"], writes=[dkey])
            items = []
            for r in range(Dg):
                for qb in range(nqb):
                    kbs = []
                    if halo or qb > 0:
                        kbs.append((r * nkc + qb - 1 + hoff, 0))
                    kbs.append((r * nkc + qb + hoff, 1))
                    for hh in range(2):
                        items.append(dict(g=g, rcl=r, cs=cs_, qb=qb, qi=r * nqb + qb, hh=hh, kbs=kbs, Dg=Dg))
            return items

        def stage_a(it):
            cs_, qb, hh, kbs = it["cs"], it["qi"], it["hh"], it["kbs"]
            hp = slice(hh * 64, (hh + 1) * 64)
            qt, kt = qT2[cs_], kT2[cs_]
            pt, pk = next_ps()
            for (kb, mi) in kbs:
                P.op("pe", lambda e, kb=kb, mi=mi: e.matmul(pt[:, mi * 128:(mi + 1) * 128], lhsT=kt[hp, kb * 128:(kb + 1) * 128], rhs=qt[hp, qb * 128:(qb + 1) * 128], start=True, stop=True),
                     reads=[("kT", cs_), ("qT", cs_)], writes=[pk], signal=(mi == 1))
            c_lo = 0 if len(kbs) == 2 else 128
            pi = clsn["it"] % NPT
            clsn["it"] += 1
            pp = pT[pi]
            it["pi"], it["pp"] = pi, pp
            P.op("act", lambda e: e.activation(out=pp[:, c_lo:256], in_=pt[:, c_lo:256], func=AF.Exp, scale=scale),
                 reads=[pk], writes=[("pT", pi)])
            P.op("pool", lambda e: e.tensor_tensor(out=pp[:, c_lo:256], in0=pp[:, c_lo:256], in1=masks[:, c_lo:256], op=ALU.mult),
                 reads=[("pT", pi), "masks"], writes=[("pT", pi)])

        def stage_b(it):
            cs_, qb, hh, kbs, g, rcl, Dg = it["cs"], it["qb"], it["hh"], it["kbs"], it["g"], it["rcl"], it["Dg"]
            pi, pp = it["pi"], it["pp"]
            hp = slice(hh * 64, (hh + 1) * 64)
            vr = vrows2[cs_]
            po, pok = next_ps()
            nk = len(kbs)
            for i, (kb, mi) in enumerate(kbs):
                P.op("pe", lambda e, kb=kb, mi=mi, i=i: e.matmul(po[hp, 0:128], lhsT=vr[:, kb, hp], rhs=pp[:, mi * 128:(mi + 1) * 128], start=(i == 0), stop=(i == nk - 1)),
                     reads=[("vrows", cs_), ("pT", pi)], writes=[pok], signal=False)
            for i, (kb, mi) in enumerate(kbs):
                P.op("pe", lambda e, mi=mi, i=i: e.matmul(po[hp, 128:256], lhsT=onesc[:, hp], rhs=pp[:, mi * 128:(mi + 1) * 128], start=(i == 0), stop=(i == nk - 1)),
                     reads=["ones", ("pT", pi)], writes=[pok], signal=(i == nk - 1))
            aOL = accOL[hp, :, :].rearrange("p a (n d) -> p a d n", d=Dg)[:, :, rcl, qb * 128:(qb + 1) * 128]
            ks_ = acc_keys("acc", g, hh, qb)
            P.op("dve", lambda e: e.tensor_tensor(out=aOL, in0=po[hp, 0:256].rearrange("p (a n) -> p a n", a=2), in1=aOL, op=ALU.add),
                 reads=[pok] + ks_, writes=ks_)

        ALLACC = [("acc", hh, t) for hh in range(2) for t in range(4)]
        for pr in range(4):
            pc = slice(pr * 128, (pr + 1) * 128)
            P.op("dve", lambda e: e.memset(accOL[:], 0.0), writes=ALLACC)
            if not sample:
                classes = [0, 1, 2]
                cur = class_setup(pc, classes[0], clsn["i"] % 2)
                clsn["i"] += 1
                for cidx in range(len(classes)):
                    nxt = None
                    if cidx + 1 < len(classes):
                        nxt = class_setup(pc, classes[cidx + 1], clsn["i"] % 2)
                        clsn["i"] += 1
                    LOOK = 3
                    for i in range(min(LOOK, len(cur))):
                        stage_a(cur[i])
                    for i in range(len(cur)):
                        stage_b(cur[i])
                        if i + LOOK < len(cur):
                            stage_a(cur[i + LOOK])
                    cur = nxt
            for g in range(3):
                Dg = DIL[g]
                if not sample:
                    pass
                else:
                    P.op("sp", lambda e, pc=pc, g=g: e.dma_start(out=qrows[0:NS, 0, :], in_=Qs[li][g][SEQ:SEQ + NS, pc]), reads=[("scr", 0, g)], writes=["qrows"], dma=True)
                    P.op("sp", lambda e, pc=pc, g=g: e.dma_start(out=knew[:, :], in_=Ks[li][g][SEQ:SEQ + NS, pc]), reads=[("scr", 1, g)], writes=["knew"], dma=True)
                    P.op("sp", lambda e, pc=pc, g=g: e.dma_start(out=vnew[0:1, :, :], in_=Vs[li][g][SEQ:SEQ + NS, pc].rearrange("(o b) c -> o b c", o=1)), reads=[("scr", 2, g)], writes=["vnew"], dma=True)
                    P.op("pe", lambda e: e.transpose(psb[:, 0:NS], qrows[0:NS, 0, :], identb[0:NS, 0:NS]), reads=["qrows", "identb"], writes=["psb"], signal=False)
                    P.op("pe", lambda e: e.transpose(psb[:, 128:128 + NS], knew[:, :], identb[0:NS, 0:NS]), reads=["knew", "identb"], writes=["psb"])
                    P.op("act", lambda e: e.activation(out=qT[:, 0:NS], in_=psb[:, 0:NS], func=AF.Copy), reads=["psb"], writes=["qT"])
                    P.op("act", lambda e: e.activation(out=qT[:, 128:128 + NS], in_=psb[:, 128:128 + NS], func=AF.Copy), reads=["psb"], writes=["qT"])
                    for b in range(NS):
                        P.op("pool", lambda e, pc=pc, g=g, b=b, Dg=Dg: e.dma_start(out=krows[:, b, :], in_=cch[g][li, b].rearrange("(n d) t c -> d n t c", d=Dg)[0, :, 0, pc]), writes=["krows"], dma=True)
                        P.op("pool", lambda e, pc=pc, g=g, b=b, Dg=Dg: e.dma_start(out=vrows[:, b, :], in_=cch[g][li, b].rearrange("(n d) t c -> d n t c", d=Dg)[0, :, 1, pc]), writes=["vrows"], dma=True)
                    for b in range(NS):
                        P.op("pe", lambda e, b=b: e.transpose(psb[:, b * 128:(b + 1) * 128], krows[:, b, :], identb[:]), reads=["krows", "identb"], writes=["psb"], signal=(b == NS - 1))
                    P.op("act", lambda e: e.activation(out=kT[:, 0:NS * 128], in_=psb[:, 0:NS * 128], func=AF.Copy), reads=["psb"], writes=["kT"])
                    for b in range(NS):
                        for hh in range(2):
                            hp = slice(hh * 64, (hh + 1) * 64)
                            pt, pk = next_ps()
                            P.op("pe", lambda e, b=b, hp=hp, pt=pt: e.matmul(pt[:, 0:1], lhsT=kT[hp, b * 128:(b + 1) * 128], rhs=qT[hp, b:b + 1], start=True, stop=True),
                                 reads=["kT", "qT"], writes=[pk], signal=False)
                            P.op("pe", lambda e, b=b, hp=hp, pt=pt: e.matmul(pt[0:1, 2:3], lhsT=qT[hp, 128 + b:129 + b], rhs=qT[hp, b:b + 1], start=True, stop=True),
                                 reads=["qT"], writes=[pk])
                            pi = (b * 2 + hh) % 2
                            pp = pT[pi]
                            P.op("act", lambda e, pt=pt, pp=pp: e.activation(out=pp[:, 0:1], in_=pt[:, 0:1], func=AF.Exp, scale=scale), reads=[pk], writes=[("pT", pi)])
                            P.op("act", lambda e, pt=pt, pp=pp: e.activation(out=pp[0:1, 2:3], in_=pt[0:1, 2:3], func=AF.Exp, scale=scale), reads=[pk], writes=[("pT", pi)])
                            po, pok = next_ps()
                            P.op("pe", lambda e, b=b, hp=hp, po=po, pp=pp: e.matmul(po[hp, 0:1], lhsT=vrows[:, b, hp], rhs=pp[:, 0:1], start=True, stop=False),
                                 reads=["vrows", ("pT", pi)], writes=[pok], signal=False)
                            P.op("pe", lambda e, b=b, hp=hp, po=po, pp=pp: e.matmul(po[hp, 0:1], lhsT=vnew[0:1, b, hp], rhs=pp[0:1, 2:3], start=False, stop=True),
                                 reads=["vnew", ("pT", pi)], writes=[pok], signal=False)
                            P.op("pe", lambda e, hp=hp, po=po, pp=pp: e.matmul(po[hp, 2:3], lhsT=onesc[:, hp], rhs=pp[:, 0:1], start=True, stop=False),
                                 reads=["ones", ("pT", pi)], writes=[pok], signal=False)
                            P.op("pe", lambda e, hp=hp, po=po, pp=pp: e.matmul(po[hp, 2:3], lhsT=onesc[0:1, hp], rhs=pp[0:1, 2:3], start=False, stop=True),
                                 reads=["ones", ("pT", pi)], writes=[pok])
                            P.op("dve", lambda e, hp=hp, po=po, b=b: e.tensor_tensor(out=accO[hp, b:b + 1], in0=po[hp, 0:1], in1=accO[hp, b:b + 1], op=ALU.add),
                                 reads=[pok] + ALLACC, writes=ALLACC)
                            P.op("dve", lambda e, hp=hp, po=po, b=b: e.tensor_tensor(out=accL[hp, b:b + 1], in0=po[hp, 2:3], in1=accL[hp, b:b + 1], op=ALU.add),
                                 reads=[pok] + ALLACC, writes=ALLACC)
            P.op("dve", lambda e: e.reciprocal(out=accL, in_=accL), reads=ALLACC, writes=ALLACC)
            P.op("dve", lambda e, pr=pr: e.tensor_tensor(out=attnT[:, pr, :], in0=accO, in1=accL, op=ALU.mult), reads=ALLACC, writes=[("attnT", pr)])
        if DBG_DUMP and li == 0 and ci == 0 and not sample:
            P.op("sp", lambda e: e.dma_start(out=dbg_attn, in_=attnT[:]), reads=[("attnT", 0), ("attnT", 1), ("attnT", 2), ("attnT", 3)], writes=["OUT"], dma=True)
        wot = alloc([128, 4, D], BF16)
        P.op("sp", lambda e: e.dma_start(out=wot[:], in_=wo_b[li].rearrange("(k p) d -> p k d", p=128)), writes=["wot"], dma=True)
        for dc in range(8):
            for ti, (c0, n) in enumerate(tiles_of(T)):
                pt, pk = next_ps()
                for kc in range(4):
                    P.op("pe", lambda e, kc=kc, dc=dc, c0=c0, n=n, pt=pt: e.matmul(pt[:, 0:n], lhsT=wot[:, kc, dc * 128:(dc + 1) * 128], rhs=attnT[:, kc, c0:c0 + n], start=(kc == 0), stop=(kc == 3)),
                         reads=["wot", ("attnT", kc)], writes=[pk], signal=(kc == 3))
                P.op("dve", lambda e, dc=dc, c0=c0, n=n, pt=pt: e.tensor_tensor(out=x[:, dc, c0:c0 + n], in0=pt[:, 0:n], in1=x[:, dc, c0:c0 + n], op=ALU.add),
                     reads=[pk, ("x", dc, ti)], writes=[("x", dc, ti)])


    def run_chunk(ci, sample):
        T = NS if sample else TC
        P.barrier(keep=CONST + ["hprev0", "hprev1"])
        if sample:
            P.op("sp", lambda e: e.dma_start(out=x[:, :, 0:NS], in_=xsT.rearrange("(k p) t -> p k t", p=128)), writes=["xload"], dma=True)
        else:
            for k in range(8):
                P.op("sp", lambda e, k=k: e.dma_start(out=x[:, k, :], in_=xT[k * 128:(k + 1) * 128, ci * TC:(ci + 1) * TC]), writes=["xload"], dma=True)
        P.barrier(keep=CONST + ["hprev0", "hprev1"])
        for l in range(DBG_LAYERS):
            if l % 2 == 0:
                pool_layer(T, l, ci, sample)
            else:
                attn_layer(T, l, ci, sample)
            if not (DBG_SKIP_LAST_MLP and l == DBG_LAYERS - 1):
                mlp(T, l)
        P.barrier(keep=CONST + ["hprev0", "hprev1"])
        if sample:
            P.op("sp", lambda e: e.dma_start(out=ysT.rearrange("(k p) t -> p k t", p=128), in_=x[:, :, 0:NS]), writes=["OUT"], dma=True)
        else:
            for k in range(8):
                P.op("sp", lambda e, k=k: e.dma_start(out=yT[k * 128:(k + 1) * 128, ci * TC:(ci + 1) * TC], in_=x[:, k, :]), writes=["OUT"], dma=True)

    for ci in range(DBG_CHUNKS):
        run_chunk(ci, False)
    if DBG_SAMPLE:
        run_chunk(0, True)
    P.barrier(final=True)

    with nc.Block() as block:
        @block.tensor
        def _(e):
            for f in P.q["pe"]:
                f(e)

        @block.scalar
        def _(e):
            for f in P.q["act"]:
                f(e)

        @block.vector
        def _(e):
            for f in P.q["dve"]:
                f(e)

        @block.gpsimd
        def _(e):
            for f in P.q["pool"]:
                f(e)

        @block.sync
        def _(e):
            for f in P.q["sp"]:
                f(e)
    return nc


def _rope_tables():
    half = 32
    inv = (10000.0 ** (-np.arange(half, dtype=np.float32) * 2.0 / 64)).astype(np.float32)
    pos = np.concatenate([np.arange(SEQ), np.full(NS, SEQ)]).astype(np.float32)
    ang = pos[:, None] * inv[None, :]
    return np.cos(ang).astype(np.float32), np.sin(ang).astype(np.float32)


def kernel(x_prompt, x_sample, state_pool, cache_kv_w128, cache_kv_w512, cache_kv_w2048,
           norm_mix, norm_mlp, pool_w, pool_scale, attn_w_qkv, attn_q_norm, attn_k_norm,
           attn_w_o, mlp_w_up, mlp_w_down):
    f = lambda a: np.ascontiguousarray(np.asarray(a, dtype=np.float32))
    x_prompt, x_sample, state_pool = f(x_prompt), f(x_sample), f(state_pool)
    caches = [f(cache_kv_w128), f(cache_kv_w512), f(cache_kv_w2048)]
    cosT, sinT = _rope_tables()
    ii = np.arange(128)[:, None]
    jj = np.arange(128)[None, :]
    masks = np.concatenate([(jj <= ii), (jj >= ii)], axis=1).astype(np.float32).astype(ml_dtypes.bfloat16)
    invc = np.zeros((128, 8, 16), np.float32)
    for k in range(8):
        w = WIN[k // 2]
        invc[:, k, :] = 1.0 / np.minimum(np.arange(16) + 1, w).astype(np.float32)
    lay = lambda g: f(np.asarray(g, np.float32).reshape(g.shape[0], 8, 128).transpose(2, 0, 1))
    qkn = np.zeros((2, 3, 2, 128, 512), np.float32)
    qn_, kn_ = f(attn_q_norm), f(attn_k_norm)
    for li in range(2):
        for g in range(3):
            qkn[li, g, 0] = np.tile(qn_[li, g], (128, 8))
            qkn[li, g, 1] = np.tile(kn_[li, g], (128, 8))
    shared = {
        "gmix": lay(f(norm_mix)), "gmlp": lay(f(norm_mlp)), "pscale": lay(f(pool_scale)),
        "pool_w": f(pool_w), "wqkv": f(attn_w_qkv), "wo": f(attn_w_o), "wup": f(mlp_w_up), "wdn": f(mlp_w_down),
        "qkn": qkn, "sintab": sinT, "costab2": np.ascontiguousarray(np.concatenate([cosT, cosT], axis=1)), "nsintab": np.ascontiguousarray(-sinT), "masks": masks, "invc": invc,
        "identb": np.eye(128, dtype=np.float32).astype(ml_dtypes.bfloat16), "identf": np.eye(128, dtype=np.float32),
    }
    in_maps = []
    for c in range(NCORES):
        m = dict(shared)
        m["xT"] = f(x_prompt[c].T) if c < 2 else np.zeros((D, SEQ), np.float32)
        sl = slice(NS * c, NS * (c + 1))
        m["xsT"] = f(x_sample[sl, 0, :].T)
        m["stT"] = f(state_pool[:, sl].transpose(0, 3, 1, 2))
        m["st"] = f(state_pool[:, sl])
        m["c128"] = f(caches[0][:, sl].reshape(2, NS, 128, 2, 512))
        m["c512"] = f(caches[1][:, sl].reshape(2, NS, 512, 2, 512))
        m["c2048"] = f(caches[2][:, sl].reshape(2, NS, 2048, 2, 512))
        in_maps.append(m)
    nc = build_nc()
    res = run_bass_kernel_spmd(nc, in_maps, core_ids=list(range(NCORES))).results
    if DBG_DUMP:
        global LAST_RES
        LAST_RES = res
    y_prompt = np.stack([res[b]["yT"].T for b in range(2)]).astype(np.float32)
    y_sample = np.concatenate([res[c]["ysT"].T for c in range(NCORES)], axis=0)[:, None, :].astype(np.float32)
    pool_p = np.stack([res[b]["poolp"] for b in range(2)], axis=1).astype(np.float32)
    pool_s = np.concatenate([res[c]["pools"] for c in range(NCORES)], axis=1).astype(np.float32)
    outs = [y_prompt, y_sample, pool_p, pool_s]
    for g, (nm, nb) in enumerate((("128", 128), ("512", 512), ("2048", 2048))):
        kp = np.stack([res[b]["kp" + nm] for b in range(2)], axis=1).reshape(2, 2, nb, 2, 8, 64)
        ks = np.concatenate([res[c]["ks" + nm] for c in range(NCORES)], axis=1).reshape(2, NCORES * NS, nb, 2, 8, 64)
        outs += [kp.astype(np.float32), ks.astype(np.float32)]
    return tuple(np.ascontiguousarray(o) for o in outs)
```
